# Optimizing a Trainium2 kernel written in Bass

```python
import jax, jax.numpy as jnp
from jax import lax
import numpy as np

D_MODEL = 1024
BATCH = 8
SEQ = 2048
DEPTH = 4
DEC_BATCH = 128
DEC_SEQ = 1
PAST_LEN = 16384
PAGE_SIZE = 128

N_MIXERS = 3
EPS = 1e-6
POOL_WINDOWS = (2, 4, 8, 16)
POOL_GROUPS = len(POOL_WINDOWS)
D_POOL = 2 * D_MODEL
POOL_GW = D_POOL // POOL_GROUPS
POOL_HIST = max(POOL_WINDOWS) - 1
SGU_CHUNK = 128
D_SG = 2 * D_MODEL
SGU_GROUPS = 4
SGU_GW = D_SG // SGU_GROUPS
DN_DK = 128
DN_DV = 128
DN_HEADS = D_MODEL // DN_DK
DN_QK = DN_HEADS * DN_DK
DN_V = DN_HEADS * DN_DV
DN_CONV_CH = 2 * DN_QK + DN_V
DN_CONV = 4
DN_CHUNK = 64
DN_PROJ = DN_CONV_CH + DN_V + 2 * DN_HEADS

kernel_name = "hybrid_pool_sgu_gdn_decoder_step"

F32 = jnp.float32


def rmsnorm(x, gain):
    xf = x.astype(F32)
    y = xf * lax.rsqrt(jnp.mean(xf * xf, axis=-1, keepdims=True) + EPS)
    return (y * gain.astype(F32)).astype(x.dtype)


def l2norm(x):
    xf = x.astype(F32)
    return xf * lax.rsqrt(jnp.sum(xf * xf, axis=-1, keepdims=True) + EPS)


def pool_mixer(h, hist, start_pos, w_in, w_grp, scale, w_out):
    b, t, _ = h.shape
    xb, z = jnp.split(h @ w_in, 2, axis=-1)
    x_ext = jnp.concatenate([hist.astype(xb.dtype), xb], axis=1)
    cs = jnp.cumsum(jnp.pad(x_ext.astype(F32), ((0, 0), (1, 0), (0, 0))), axis=1)
    pos = start_pos + jnp.arange(t)
    means = []
    for gi, w in enumerate(POOL_WINDOWS):
        c0, c1 = gi * POOL_GW, (gi + 1) * POOL_GW
        win = cs[:, POOL_HIST + 1:POOL_HIST + 1 + t, c0:c1] - cs[:, POOL_HIST + 1 - w:POOL_HIST + 1 - w + t, c0:c1]
        cnt = jnp.minimum(pos + 1, w).astype(F32)
        means.append(win / cnt[None, :, None])
    pooled = jnp.concatenate(means, axis=-1) - xb.astype(F32)
    mixed = jnp.einsum('btgc,gcd->btgd', pooled.reshape(b, t, POOL_GROUPS, POOL_GW), w_grp.astype(F32))
    mixed = mixed.reshape(b, t, D_POOL) * scale.astype(F32) * jax.nn.silu(z.astype(F32))
    y = mixed.astype(h.dtype) @ w_out
    return y, x_ext[:, -POOL_HIST:]


def sgu_mixer(h, w_in, ln_g, ln_b, w_s, b_s, w_out):
    b, t, _ = h.shape
    u, v, z = jnp.split(h @ w_in, 3, axis=-1)
    u = jax.nn.gelu(u.astype(F32), approximate=False)
    v = jax.nn.gelu(v.astype(F32), approximate=False)
    mu = jnp.mean(v, axis=-1, keepdims=True)
    var = jnp.mean(jnp.square(v - mu), axis=-1, keepdims=True)
    v = (v - mu) * lax.rsqrt(var + EPS) * ln_g.astype(F32) + ln_b.astype(F32)
    n = -(-t // SGU_CHUNK)
    vp = jnp.pad(v, ((0, 0), (0, n * SGU_CHUNK - t), (0, 0))).reshape(b, n, SGU_CHUNK, SGU_GROUPS, SGU_GW)
    causal = jnp.tril(jnp.ones((SGU_CHUNK, SGU_CHUNK), dtype=bool))
    ws = jnp.where(causal, w_s.astype(F32), 0.0)
    s = jnp.einsum('gij,bnjgc->bnigc', ws, vp) + b_s.astype(F32).T[None, None, :, :, None]
    s = s.reshape(b, n * SGU_CHUNK, D_SG)[:, :t]
    y = (u * s * jax.nn.silu(z.astype(F32))).astype(h.dtype) @ w_out
    return y, v.astype(h.dtype)


def gated_delta_chunked(q, k, v, g, beta, s0):
    b, t = q.shape[:2]
    n = -(-t // DN_CHUNK)
    pad = n * DN_CHUNK - t

    def to_chunks(a):
        a = jnp.pad(a, ((0, 0), (0, pad)) + ((0, 0),) * (a.ndim - 2))
        a = a.reshape((b, n, DN_CHUNK) + a.shape[2:])
        return jnp.moveaxis(jnp.moveaxis(a, 3, 2), 1, 0)

    qc, kc, vc, gc, bc = [to_chunks(a) for a in (q, k, v, g, beta)]
    gam = jnp.cumsum(gc, axis=-1)
    idx = jnp.arange(DN_CHUNK)
    incl = idx[:, None] >= idx[None, :]
    strict = idx[:, None] > idx[None, :]
    dec = jnp.exp(jnp.where(incl, gam[..., :, None] - gam[..., None, :], -jnp.inf))
    m = jnp.where(strict, bc[..., :, None] * jnp.einsum('nbhid,nbhjd->nbhij', kc, kc) * dec, 0.0)
    rhs = jnp.concatenate([bc[..., None] * vc, (bc * jnp.exp(gam))[..., None] * kc], axis=-1)
    sol = lax.linalg.triangular_solve(m + jnp.eye(DN_CHUNK, dtype=m.dtype), rhs,
                                      left_side=True, lower=True, unit_diagonal=True)
    u_val, w_k = sol[..., :DN_DV], sol[..., DN_DV:]
    qk = jnp.einsum('nbhid,nbhjd->nbhij', qc, kc) * dec
    q_dec = qc * jnp.exp(gam)[..., None]
    k_tail = kc * jnp.exp(gam[..., -1:] - gam)[..., None]
    a_last = jnp.exp(gam[..., -1])

    def step(s, xs):
        u_c, wk_c, qk_c, qd_c, kt_c, al_c = xs
        w = u_c - jnp.einsum('bhck,bhkv->bhcv', wk_c, s)
        o = jnp.einsum('bhck,bhkv->bhcv', qd_c, s) + jnp.einsum('bhij,bhjv->bhiv', qk_c, w)
        s = al_c[..., None, None] * s + jnp.einsum('bhck,bhcv->bhkv', kt_c, w)
        return s, o

    s_fin, o = lax.scan(step, s0.astype(F32), (u_val, w_k, qk, q_dec, k_tail, a_last))
    o = jnp.moveaxis(jnp.moveaxis(o, 0, 1), 2, 3)
    o = o.reshape(b, n * DN_CHUNK, DN_HEADS, DN_DV)[:, :t]
    return o, s_fin


def delta_mixer(h, conv_hist, s0, w_in, conv_w, a_log, dt_bias, o_gain, w_out):
    b, t, _ = h.shape
    proj = h @ w_in
    qkv = proj[..., :DN_CONV_CH]
    z = proj[..., DN_CONV_CH:DN_CONV_CH + DN_V]
    a = proj[..., DN_CONV_CH + DN_V:DN_CONV_CH + DN_V + DN_HEADS]
    bb = proj[..., DN_CONV_CH + DN_V + DN_HEADS:]
    x_ext = jnp.concatenate([conv_hist.astype(qkv.dtype), qkv], axis=1)
    cw = conv_w.astype(F32)
    conv = sum(x_ext[:, j:j + t].astype(F32) * cw[j] for j in range(DN_CONV))
    conv = jax.nn.silu(conv)
    q = l2norm(conv[..., :DN_QK].reshape(b, t, DN_HEADS, DN_DK)) * (DN_DK ** -0.5)
    k = l2norm(conv[..., DN_QK:2 * DN_QK].reshape(b, t, DN_HEADS, DN_DK))
    v = conv[..., 2 * DN_QK:].reshape(b, t, DN_HEADS, DN_DV)
    g = -jnp.exp(a_log.astype(F32)) * jax.nn.softplus(a.astype(F32) + dt_bias.astype(F32))
    beta = jax.nn.sigmoid(bb.astype(F32))
    o, s_new = gated_delta_chunked(q, k, v, g, beta, s0)
    o = rmsnorm(o, o_gain).reshape(b, t, DN_V) * jax.nn.silu(z.astype(F32))
    y = o.astype(h.dtype) @ w_out
    return y, x_ext[:, -(DN_CONV - 1):], s_new


def setup_inputs(seed: int = 0) -> dict:
    key = jax.random.key(seed)
    ks = iter(jax.random.split(key, 64))

    def nrm(shape, scale):
        return jax.random.normal(next(ks), shape, F32) * scale

    def gain(n):
        return 1.0 + nrm((n,), 0.02)

    def pool_params():
        return (nrm((D_MODEL, 2 * D_POOL), D_MODEL ** -0.5),
                nrm((POOL_GROUPS, POOL_GW, POOL_GW), POOL_GW ** -0.5),
                gain(D_POOL),
                nrm((D_POOL, D_MODEL), D_POOL ** -0.5))

    inp = {}
    inp["x_prompt"] = nrm((BATCH, SEQ, D_MODEL), 1.0)
    inp["x_sample"] = nrm((DEC_BATCH, DEC_SEQ, D_MODEL), 1.0)
    inp["state_pool_l0"] = nrm((DEC_BATCH, POOL_HIST, D_POOL), 1.0)
    inp["state_conv_l2"] = nrm((DEC_BATCH, DN_CONV - 1, DN_CONV_CH), 1.0)
    inp["state_delta_l2"] = nrm((DEC_BATCH, DN_HEADS, DN_DK, DN_DV), 0.3)
    inp["state_pool_l3"] = nrm((DEC_BATCH, POOL_HIST, D_POOL), 1.0)

    inp["l0_norm"] = gain(D_MODEL)
    p = pool_params()
    inp["l0_pool_w_in"], inp["l0_pool_w_grp"], inp["l0_pool_scale"], inp["l0_pool_w_out"] = p

    inp["l1_norm"] = gain(D_MODEL)
    inp["l1_sgu_w_in"] = nrm((D_MODEL, 3 * D_SG), D_MODEL ** -0.5)
    inp["l1_sgu_ln_g"] = gain(D_SG)
    inp["l1_sgu_ln_b"] = nrm((D_SG,), 0.02)
    inp["l1_sgu_w_s"] = nrm((SGU_GROUPS, SGU_CHUNK, SGU_CHUNK), SGU_CHUNK ** -0.5)
    inp["l1_sgu_b_s"] = nrm((SGU_GROUPS, SGU_CHUNK), 0.1)
    inp["l1_sgu_w_out"] = nrm((D_SG, D_MODEL), D_SG ** -0.5)

    inp["l2_norm"] = gain(D_MODEL)
    inp["l2_dn_w_in"] = nrm((D_MODEL, DN_PROJ), D_MODEL ** -0.5)
    inp["l2_dn_conv_w"] = nrm((DN_CONV, DN_CONV_CH), DN_CONV ** -0.5)
    inp["l2_dn_a_log"] = jnp.log(jax.random.uniform(next(ks), (DN_HEADS,), F32, 1.0, 16.0))
    dt = jnp.exp(jax.random.uniform(next(ks), (DN_HEADS,), F32) * (jnp.log(0.1) - jnp.log(0.001)) + jnp.log(0.001))
    inp["l2_dn_dt_bias"] = dt + jnp.log(-jnp.expm1(-dt))
    inp["l2_dn_o_gain"] = gain(DN_DV)
    inp["l2_dn_w_out"] = nrm((DN_V, D_MODEL), DN_V ** -0.5)

    inp["l3_norm"] = gain(D_MODEL)
    p = pool_params()
    inp["l3_pool_w_in"], inp["l3_pool_w_grp"], inp["l3_pool_scale"], inp["l3_pool_w_out"] = p

    inp["final_norm"] = gain(D_MODEL)
    return inp


def reference(x_prompt, x_sample, state_pool_l0, state_conv_l2, state_delta_l2, state_pool_l3,
              l0_norm, l0_pool_w_in, l0_pool_w_grp, l0_pool_scale, l0_pool_w_out,
              l1_norm, l1_sgu_w_in, l1_sgu_ln_g, l1_sgu_ln_b, l1_sgu_w_s, l1_sgu_b_s, l1_sgu_w_out,
              l2_norm, l2_dn_w_in, l2_dn_conv_w, l2_dn_a_log, l2_dn_dt_bias, l2_dn_o_gain, l2_dn_w_out,
              l3_norm, l3_pool_w_in, l3_pool_w_grp, l3_pool_scale, l3_pool_w_out,
              final_norm):
    norms = (l0_norm, l1_norm, l2_norm, l3_norm)
    layer_params = (
        (l0_pool_w_in, l0_pool_w_grp, l0_pool_scale, l0_pool_w_out),
        (l1_sgu_w_in, l1_sgu_ln_g, l1_sgu_ln_b, l1_sgu_w_s, l1_sgu_b_s, l1_sgu_w_out),
        (l2_dn_w_in, l2_dn_conv_w, l2_dn_a_log, l2_dn_dt_bias, l2_dn_o_gain, l2_dn_w_out),
        (l3_pool_w_in, l3_pool_w_grp, l3_pool_scale, l3_pool_w_out),
    )
    sample_states = ((state_pool_l0,), (), (state_conv_l2, state_delta_l2), (state_pool_l3,))
    bp = x_prompt.shape[0]
    xp, xs = x_prompt, x_sample
    new_states = []
    for i in range(DEPTH):
        hp = rmsnorm(xp, norms[i])
        hs = rmsnorm(xs, norms[i])
        prm = layer_params[i]
        kind = i % N_MIXERS
        if kind == 0:
            hist0 = jnp.zeros((bp, POOL_HIST, D_POOL), xp.dtype)
            yp, pool_p = pool_mixer(hp, hist0, 0, *prm)
            ys, pool_s = pool_mixer(hs, sample_states[i][0], PAST_LEN, *prm)
            new_states.append((pool_p, pool_s))
        elif kind == 1:
            yp, _ = sgu_mixer(hp, *prm)
            ys, v_s = sgu_mixer(hs, *prm)
            new_states.append((v_s,))
        else:
            conv0 = jnp.zeros((bp, DN_CONV - 1, DN_CONV_CH), xp.dtype)
            s0 = jnp.zeros((bp, DN_HEADS, DN_DK, DN_DV), F32)
            yp, conv_p, dn_p = delta_mixer(hp, conv0, s0, *prm)
            ys, conv_s, dn_s = delta_mixer(hs, sample_states[i][0], sample_states[i][1], *prm)
            new_states.append((conv_p, conv_s, dn_p, dn_s))
        xp = xp + yp
        xs = xs + ys
    y_prompt = rmsnorm(xp, final_norm)
    y_sample = rmsnorm(xs, final_norm)
    (pool0_p, pool0_s), (sgu1_s,), (conv2_p, conv2_s, dn2_p, dn2_s), (pool3_p, pool3_s) = new_states
    return (y_prompt, y_sample, pool0_p, pool0_s, sgu1_s, conv2_p, conv2_s, dn2_p, dn2_s, pool3_p, pool3_s)
```

```python
import numpy as np
import concourse.bass as bass
import concourse.mybir as mybir
from concourse.bass_utils import run_bass_kernel_spmd

F32 = mybir.dt.float32
BF16 = mybir.dt.bfloat16
U8 = mybir.dt.uint8
AF = mybir.ActivationFunctionType
ALU = mybir.AluOpType
EPS = 1e-6
ENGS = ['pe', 'act', 'dve', 'pool', 'sp']
EI = {e: i for i, e in enumerate(ENGS)}
DMA_ROW = 5
BLK = 32
NDSEM = 8
ESZ = {F32: 4, BF16: 2, U8: 1}

NCOL = 1040
SEQ = 2048
PASS = 1024
NS = 16


def _esize(dt):
    return ESZ[dt]


class Prog:
    def __init__(self, nc, arenas):
        self.nc = nc
        self.arenas = arenas
        self.ops = {e: [] for e in ENGS}
        self.Wr = {a: np.zeros((6, (b + BLK - 1) // BLK + 1), np.int64) for a, b in arenas.items()}
        self.Rd = {a: np.zeros((6, (b + BLK - 1) // BLK + 1), np.int64) for a, b in arenas.items()}
        self.ndma = 0
        self.dma_info = []
        self.dma_byq = {'sp': [], 'pool': []}
        self.tokcache = {}

    def toks(self, ap):
        name = ap.tensor.name
        if name not in self.arenas:
            return None
        key = (name, ap.offset, ap.ap, str(ap.dtype))
        r = self.tokcache.get(key)
        if r is not None:
            return r
        es = _esize(ap.dtype)
        dims = list(ap.ap)
        pstride = dims[0][0]
        off = ap.offset % pstride if pstride else ap.offset
        free = dims[1:]
        ivs = []

        def rec(d, o):
            s, n = free[d]
            if d == len(free) - 1:
                if s == 0 or n == 1:
                    ivs.append((o, o + 1))
                else:
                    ivs.append((o, o + (n - 1) * s + 1))
                return
            for i in range(n if s != 0 else 1):
                rec(d + 1, o + i * s)
        if free:
            rec(0, off)
        else:
            ivs.append((off, off + 1))
        blks = sorted(set((lo * es // BLK, (hi * es + BLK - 1) // BLK) for lo, hi in ivs))
        merged = []
        for lo, hi in blks:
            if merged and lo <= merged[-1][1]:
                merged[-1][1] = max(merged[-1][1], hi)
            else:
                merged.append([lo, hi])
        r = (name, [(a, b) for a, b in merged])
        self.tokcache[key] = r
        return r

    def add(self, eng, fn, reads, writes, dma=False):
        ei = EI[eng]
        idx = len(self.ops[eng]) + 1
        raw = np.zeros(6, np.int64)
        oth = np.zeros(6, np.int64)
        dmadeps = set()
        rt = [t for t in (self.toks(a) for a in reads if a is not None) if t]
        wt = [t for t in (self.toks(a) for a in writes if a is not None) if t]
        BB = 2048 // BLK
        pw = []
        for name, ivs in rt + wt:
            if name == 'psum':
                pw.append((name, sorted(set((lo // BB * BB, (hi + BB - 1) // BB * BB) for lo, hi in ivs))))
        rt = [t for t in rt if t[0] != 'psum']
        wt = [t for t in wt if t[0] != 'psum'] + pw
        for name, ivs in rt:
            W = self.Wr[name]
            for lo, hi in ivs:
                seg = W[:, lo:hi]
                raw = np.maximum(raw, seg.max(axis=1))
                if seg[DMA_ROW].any():
                    dmadeps.update(np.unique(seg[DMA_ROW]).tolist())
        for name, ivs in wt:
            W = self.Wr[name]
            R = self.Rd[name]
            for lo, hi in ivs:
                seg = W[:, lo:hi]
                seg2 = R[:, lo:hi]
                raw = np.maximum(raw, seg.max(axis=1))
                oth = np.maximum(oth, seg2.max(axis=1))
                if seg[DMA_ROW].any():
                    dmadeps.update(np.unique(seg[DMA_ROW]).tolist())
                if seg2[DMA_ROW].any():
                    dmadeps.update(np.unique(seg2[DMA_ROW]).tolist())
        dmadeps.discard(0)
        waits = {}
        for e in ENGS:
            j = EI[e]
            if e == eng and not dma:
                if eng == 'pe':
                    continue
                v = int(max(raw[j], oth[j]))
                if v and idx - v <= 16:
                    waits[e] = v
                continue
            v = int(max(raw[j], oth[j]))
            if v:
                waits[e] = v
        dn = None
        if dma:
            dn = self.ndma
            self.ndma += 1
            lst = self.dma_byq[eng]
            self.dma_info.append((eng, len(lst)))
            if len(lst) >= NDSEM:
                dmadeps.add(lst[len(lst) - NDSEM] + 1)
            lst.append(dn)
            row, val = DMA_ROW, dn + 1
        else:
            row, val = ei, idx
        for name, ivs in rt:
            R = self.Rd[name]
            for lo, hi in ivs:
                R[row, lo:hi] = val
        for name, ivs in wt:
            W = self.Wr[name]
            for lo, hi in ivs:
                W[row, lo:hi] = val
        import sys as _sys
        fr = _sys._getframe(2)
        if fr.f_code.co_name in ('mm', 'tt', 'ts', 'stt', 'copy', 'act', 'memset', 'dma', 'transpose', 'recip', 'raw'):
            fr = fr.f_back
        self.ops[eng].append(dict(fn=fn, waits=waits, dwaits=sorted(d - 1 for d in dmadeps), dma=dn,
                                  line=fr.f_lineno))

    def mm(self, out, lhsT, rhs, start=True, stop=True):
        self.add('pe', lambda e: e.matmul(out, lhsT, rhs, start=start, stop=stop), [lhsT, rhs], [out])

    def transpose(self, out, in_, ident):
        self.add('pe', lambda e: e.transpose(out, in_, ident), [in_, ident], [out])

    def act(self, out, in_, func, scale=None, bias=None, accum_out=None, eng='act'):
        kw = {}
        rd = [in_]
        if scale is not None:
            kw['scale'] = scale
            if not isinstance(scale, (int, float)):
                rd.append(scale)
        if bias is not None:
            kw['bias'] = bias
            if not isinstance(bias, (int, float)):
                rd.append(bias)
        wr = [out]
        if accum_out is not None:
            kw['accum_out'] = accum_out
            wr.append(accum_out)
        self.add('act', lambda e: e.activation(out, in_, func, **kw), rd, wr)

    def tt(self, eng, out, in0, in1, op):
        self.add(eng, lambda e: e.tensor_tensor(out, in0, in1, op), [in0, in1], [out])

    def ts(self, eng, out, in0, s1, s2, op0, op1=None, accum_out=None):
        rd = [in0] + [s for s in (s1, s2) if s is not None and not isinstance(s, (int, float))]
        wr = [out] + ([accum_out] if accum_out is not None else [])
        kw = {}
        if op1 is not None:
            kw['op1'] = op1
        if accum_out is not None:
            kw['accum_out'] = accum_out
        self.add(eng, lambda e: e.tensor_scalar(out, in0, s1, s2, op0, **kw), rd, wr)

    def stt(self, eng, out, in0, scalar, in1, op0, op1):
        rd = [in0, in1] + ([scalar] if not isinstance(scalar, (int, float)) else [])
        eng = 'dve'
        self.add(eng, lambda e: e.scalar_tensor_tensor(out, in0, scalar, in1, op0, op1), rd, [out])

    def copy(self, eng, out, in_):
        if eng == 'act':
            self.add('act', lambda e: e.copy(out, in_), [in_], [out])
        else:
            self.add(eng, lambda e: e.tensor_copy(out, in_), [in_], [out])

    def memset(self, eng, ap, val):
        self.add(eng, lambda e: e.memset(ap, val), [], [ap])

    def dma(self, eng, out, in_):
        self.add(eng, lambda e: e.dma_start(out=out, in_=in_), [in_], [out], dma=True)

    def rsqrt(self, out, in_, scale, eps_ap):
        self.act(out, in_, AF.Ln, scale=scale, bias=eps_ap)
        self.act(out, out, AF.Exp, scale=-0.5)

    def recip(self, out, in_):
        self.add('dve', lambda e: e.reciprocal(out, in_), [in_], [out])

    def raw(self, eng, fn, reads, writes):
        self.add(eng, fn, reads, writes)

    def emit(self, stack):
        nc = self.nc
        need = {e: set() for e in ENGS}
        for e in ENGS:
            for op in self.ops[e]:
                for E, j in op['waits'].items():
                    need[E].add(j)
        cum = {}
        for e in ENGS:
            c = np.zeros(len(self.ops[e]) + 1, np.int64)
            k = 0
            for i in range(1, len(self.ops[e]) + 1):
                if i in need[e]:
                    k += 1
                c[i] = k
            cum[e] = c
        print("ops:", {e: len(self.ops[e]) for e in ENGS}, "signals:", {e: int(cum[e][-1]) for e in ENGS}, "ndma:", self.ndma)
        sems = {e: stack.enter_context(nc.semaphore("s_" + e)) for e in ENGS}
        dsems = {q: [stack.enter_context(nc.semaphore("d%s%d" % (q, i))) for i in range(NDSEM)]
                 for q in ('sp', 'pool')}
        dinfo = self.dma_info
        dbyq = self.dma_byq
        block = stack.enter_context(nc.Block())
        ndma = self.ndma
        prog = self

        def run(eng_name, e):
            waited = {E: 0 for E in ENGS}
            dwaited = {q: [0] * NDSEM for q in ('sp', 'pool')}
            ops = prog.ops[eng_name]
            for i, op in enumerate(ops, start=1):
                for E, j in op['waits'].items():
                    v = int(cum[E][j])
                    if v > waited[E]:
                        e.wait_ge(sems[E], v)
                        waited[E] = v
                for dn in op['dwaits']:
                    q, k = dinfo[dn]
                    s = k % NDSEM
                    v = 16 * (k // NDSEM + 1)
                    if v > dwaited[q][s]:
                        e.wait_ge(dsems[q][s], v)
                        dwaited[q][s] = v
                ins = op['fn'](e)
                if op['dma'] is not None:
                    q, k = dinfo[op['dma']]
                    ins.then_inc(dsems[q][k % NDSEM], 16)
                elif i in need[eng_name]:
                    ins.then_inc(sems[eng_name], 1)
            if eng_name == 'sp':
                for q in ('sp', 'pool'):
                    nq = len(dbyq[q])
                    for s in range(NDSEM):
                        cnt = len(range(s, nq, NDSEM))
                        if cnt and 16 * cnt > dwaited[q][s]:
                            e.wait_ge(dsems[q][s], 16 * cnt)

        @block.tensor
        def _(e):
            run('pe', e)

        @block.scalar
        def _(e):
            run('act', e)

        @block.vector
        def _(e):
            run('dve', e)

        @block.gpsimd
        def _(e):
            run('pool', e)

        @block.sync
        def _(e):
            run('sp', e)


class SB:
    def __init__(self, arena_ap, total):
        self.a = arena_ap
        self.total = total
        self.off = 0

    def alloc(self, nbytes):
        o = self.off
        self.off = (o + nbytes + 63) // 64 * 64
        assert self.off <= self.total, ("SBUF overflow", self.off, self.total)
        return o

    def view(self, off, dt, shape, parts=128, p0=0):
        n = int(np.prod(shape))
        v = self.a[p0:p0 + parts, off:off + n * _esize(dt)].bitcast(dt)
        if len(shape) == 2:
            v = v.rearrange("p (a b) -> p a b", a=shape[0])
        elif len(shape) == 3:
            v = v.rearrange("p (a b c) -> p a b c", a=shape[0], b=shape[1])
        return v

    def new(self, dt, shape, parts=128):
        off = self.alloc(int(np.prod(shape)) * _esize(dt))
        return self.view(off, dt, shape, parts)


ARENA_BYTES = 212000


class Builder:
    def __init__(self, n_layers=4, n_pass=2):
        self.n_layers = n_layers
        self.n_pass = n_pass

    def dram_in(self, name, shape):
        return self.nc.dram_tensor(name, list(shape), F32, kind="ExternalInput").ap()

    def dram_out(self, name, shape):
        return self.nc.dram_tensor(name, list(shape), F32, kind="ExternalOutput").ap()

    def build(self):
        from contextlib import ExitStack
        nc = bass.Bass("TRN2", target_bir_lowering=False)
        self.nc = nc
        D = {}
        D['xp'] = self.dram_in('xp', (SEQ, 1024))
        D['xs'] = self.dram_in('xs', (NS, 1024))
        D['sp0'] = self.dram_in('sp0', (NS, 15, 2048))
        D['sc2'] = self.dram_in('sc2', (NS, 3, 3072))
        D['sd2'] = self.dram_in('sd2', (NS, 8, 128, 128))
        D['sp3'] = self.dram_in('sp3', (NS, 15, 2048))
        for i in range(4):
            D['l%d_norm' % i] = self.dram_in('l%d_norm' % i, (1024,))
        D['final_norm'] = self.dram_in('final_norm', (1024,))
        for i in (0, 3):
            D['l%d_w_in' % i] = self.dram_in('l%d_w_in' % i, (1024, 4096))
            D['l%d_w_grp' % i] = self.dram_in('l%d_w_grp' % i, (4, 512, 512))
            D['l%d_scale' % i] = self.dram_in('l%d_scale' % i, (2048,))
            D['l%d_w_out' % i] = self.dram_in('l%d_w_out' % i, (2048, 1024))
        D['l1_w_in'] = self.dram_in('l1_w_in', (1024, 6144))
        D['l1_ln_g'] = self.dram_in('l1_ln_g', (2048,))
        D['l1_ln_b'] = self.dram_in('l1_ln_b', (2048,))
        D['l1_w_s'] = self.dram_in('l1_w_s', (4, 128, 128))
        D['l1_b_s'] = self.dram_in('l1_b_s', (4, 128))
        D['l1_w_out'] = self.dram_in('l1_w_out', (2048, 1024))
        D['l2_w_in'] = self.dram_in('l2_w_in', (1024, 4112))
        D['l2_conv_w'] = self.dram_in('l2_conv_w', (4, 3072))
        D['l2_a_log'] = self.dram_in('l2_a_log', (8,))
        D['l2_dt_bias'] = self.dram_in('l2_dt_bias', (8,))
        D['l2_o_gain'] = self.dram_in('l2_o_gain', (128,))
        D['l2_w_out'] = self.dram_in('l2_w_out', (1024, 1024))
        O = {}
        O['yp'] = self.dram_out('yp', (SEQ, 1024))
        O['ys'] = self.dram_out('ys', (NS, 1024))
        O['p0p'] = self.dram_out('p0p', (15, 2048))
        O['p0s'] = self.dram_out('p0s', (NS, 15, 2048))
        O['v1s'] = self.dram_out('v1s', (NS, 2048))
        O['c2p'] = self.dram_out('c2p', (3, 3072))
        O['c2s'] = self.dram_out('c2s', (NS, 3, 3072))
        O['d2p'] = self.dram_out('d2p', (8, 128, 128))
        O['d2s'] = self.dram_out('d2s', (NS, 8, 128, 128))
        O['p3p'] = self.dram_out('p3p', (15, 2048))
        O['p3s'] = self.dram_out('p3s', (NS, 15, 2048))
        self.D, self.O = D, O

        with ExitStack() as stack:
            arena = stack.enter_context(nc.sbuf_tensor("sb", [128, ARENA_BYTES], U8))
            psum = stack.enter_context(nc.psum_tensor("psum", [128, 4096], F32))
            self.P = Prog(nc, {"sb": ARENA_BYTES, "psum": 16384})
            self.sb = SB(arena, ARENA_BYTES)
            self.psum = psum
            self.pbank = 0
            self.program()
            self.P.emit(stack)
        return nc

    def bank(self, n=1):
        nb = getattr(self, 'bank_n', 8)
        b = self.pbank
        if b % n:
            b += n - b % n
        if b + n > nb:
            b = 0
        self.pbank = (b + n) % nb
        return self.psum[:, b * 512:(b + n) * 512]

    def program(self):
        P, sb, D, O = self.P, self.sb, self.D, self.O
        self.xT = sb.new(F32, [8, NCOL])
        self.hT = sb.new(BF16, [8, NCOL])
        self.B1 = sb.new(BF16, [16, NCOL])
        self.Roff = sb.alloc(61440)
        self.Woff = [sb.alloc(16384) for _ in range(3)]
        self.wslot_i = 0
        self.ident = sb.new(F32, [128])
        self.identb = sb.new(BF16, [128])
        self.onesb = sb.new(BF16, [128])
        self.halo = {0: sb.new(F32, [16, 15]), 3: sb.new(F32, [16, 15])}
        self.gain = {i: sb.new(F32, [8]) for i in range(4)}
        self.pscale = {0: sb.new(F32, [16]), 3: sb.new(F32, [16])}
        self.stage_small = sb.new(F32, [128])
        self.epsc = sb.new(F32, [1])
        self.rs_n = sb.new(F32, [512])
        print("SBUF persistent bytes:", sb.off)
        P.memset('pool', self.epsc, EPS)

        P.memset('pool', self.ident, 1.0)
        P.raw('pool', lambda e: e.affine_select(out=self.ident, in_=self.ident, pattern=[[-1, 128]],
                                               compare_op=ALU.is_equal, fill=0.0, base=0, channel_multiplier=1),
              [self.ident], [self.ident])
        P.copy('pool', self.identb, self.ident)
        P.memset('pool', self.onesb, 1.0)
        for i in range(4):
            self.load_pm(D['l%d_norm' % i], 1024, self.gain[i])
        for i in (0, 3):
            self.load_pm(D['l%d_scale' % i], 2048, self.pscale[i])

        if self.n_layers >= 2:
            self.sgu_setup()
        if self.n_layers >= 3:
            self.dn_setup()
        for p in range(self.n_pass):
            self.p = p
            self.tiles = [(0, 512), (512, 512)] + ([(1024, NS)] if p == 0 else [])
            self.load_x(p)
            for li in range(self.n_layers):
                self.norm_li = li
                self.norm_done = set()
                self.ensure_h(0)
                if li in (0, 3):
                    self.pool_layer(li)
                elif li == 1:
                    self.sgu_layer()
                else:
                    self.dn_layer()
            self.final_out(p)

    def Rview(self, off, dt, shape, parts=128, p0=0):
        return self.sb.view(self.Roff + off, dt, shape, parts, p0)

    def load_pm(self, dram1d, n, dest):
        P = self.P
        r = n // 128
        st = self.stage_small[0:r, :]
        P.dma('sp', st, dram1d.rearrange("(c p) -> c p", p=128))
        ps = self.bank()
        P.transpose(ps[:, 0:r], st, self.ident[0:r, 0:r])
        P.copy('dve', dest, ps[:, 0:r])

    def wslot(self):
        i = self.wslot_i
        self.wslot_i = (i + 1) % 3
        return self.Woff[i]

    def load_x(self, p):
        P = self.P
        st_off = 0
        for tc in range(8):
            st = self.Rview(32768 + (tc % 2) * 4096, F32, [1024])
            P.dma('sp', st, self.D['xp'][p * PASS + tc * 128: p * PASS + (tc + 1) * 128, :])
            ps = self.bank(2)
            for k in range(8):
                P.transpose(ps[:, k * 128:(k + 1) * 128], st[:, k * 128:(k + 1) * 128], self.ident)
            P.copy('act' if tc % 2 else 'dve', self.xT[:, :, tc * 128:(tc + 1) * 128],
                   ps.rearrange("p (k t) -> p k t", k=8))
        if p == 0:
            st = self.Rview(8192, F32, [1024], parts=NS)
            P.dma('sp', st, self.D['xs'])
            ps = self.bank(1)
            for k in range(8):
                P.transpose(ps[:, k * NS:(k + 1) * NS], st[:, k * 128:(k + 1) * 128], self.ident[0:NS, 0:NS])
            P.copy('dve', self.xT[:, :, 1024:1024 + NS], ps[:, 0:8 * NS].rearrange("p (k t) -> p k t", k=8))

    def ensure_h(self, ti):
        if ti in self.norm_done:
            return
        self.norm_done.add(ti)
        P = self.P
        li = self.norm_li
        c0, n = self.tiles[ti]
        sq = self.B1[:, 0:8, c0:c0 + n]
        rs = self.rs_n
        P.act(sq, self.xT[:, :, c0:c0 + n], AF.Square)
        ps = self.bank()
        for k in range(8):
            P.mm(ps[:, 0:n], self.onesb, sq[:, k, :], start=(k == 0), stop=(k == 7))
        P.rsqrt(rs[:, 0:n], ps[:, 0:n], 1.0 / 1024, self.epsc[:, 0:1])
        for k in range(8):
            P.stt('dve', self.hT[:, k, c0:c0 + n], self.xT[:, k, c0:c0 + n],
                  self.gain[li][:, k:k + 1], rs[:, 0:n], ALU.mult, ALU.mult)

    def pool_layer(self, li):
        P, D, O = self.P, self.D, self.O
        w_in = D['l%d_w_in' % li].rearrange("(k p) n -> p k n", p=128)
        w_out = D['l%d_w_out' % li].rearrange("(k p) n -> p k n", p=128)
        scale = self.pscale[li]
        halo = self.halo[li]
        st_in = D['sp0' if li == 0 else 'sp3']
        st_out_p = O['p0p' if li == 0 else 'p3p']
        st_out_s = O['p0s' if li == 0 else 'p3s']
        XE = 4 * 527 * 4
        xoff, taoff, tboff = 0, 8448, 16896
        szoff = 25344
        pooff = 33536
        hsoff = 37632
        pending = None
        for g in range(4):
            w = 2 << g
            wo = self.wslot()
            ws = self.sb.view(wo, BF16, [8, 1024])
            P.dma('pool', ws[:, :, 0:512], w_in[:, :, g * 512:(g + 1) * 512])
            P.dma('pool', ws[:, :, 512:1024], w_in[:, :, 2048 + g * 512:2048 + (g + 1) * 512])
            go = self.wslot()
            wg = self.sb.view(go, BF16, [4, 512])
            P.dma('pool', wg, D['l%d_w_grp' % li][g].rearrange("(k p) n -> p k n", p=128))
            for ti, (c0, n) in enumerate(self.tiles):
                self.ensure_h(ti)
                samp = (n == NS)
                if samp:
                    nseq, T = NS, 1
                else:
                    nseq, T = 1, n
                L = 15 + T
                alt = (g * len(self.tiles) + ti) % 2

                def V4(off, dt=F32):
                    return self.Rview(off, dt, [4, nseq, L])
                xe, ta, tb = V4(39680 if alt else xoff), V4(taoff), V4(tboff)
                sz = self.Rview(48128 if alt else szoff, F32, [4, 512])
                po = self.Rview(56320 if alt else pooff, BF16, [4, 512])
                if samp:
                    for half in range(2):
                        hs = self.Rview(hsoff, F32, [512], parts=120)
                        P.dma('sp', hs, st_in[half * 8:(half + 1) * 8, :, g * 512:(g + 1) * 512]
                              .rearrange("b r c -> (b r) c"))
                        ps = self.bank()
                        for j in range(4):
                            P.transpose(ps[:, j * 120:(j + 1) * 120], hs[:, j * 128:(j + 1) * 128],
                                        self.ident[0:120, 0:120])
                        P.copy('dve', xe[:, :, half * 8:(half + 1) * 8, 0:15],
                               ps[:, 0:480].rearrange("p (j b r) -> p j b r", j=4, b=8))
                else:
                    if self.p == 0 and ti == 0:
                        P.memset('pool', xe[:, :, :, 0:15], 0.0)
                    else:
                        P.copy('pool', xe[:, :, 0, 0:15], halo[:, g * 4:(g + 1) * 4, :])
                for j in range(4):
                    ps = self.bank()
                    for k in range(8):
                        P.mm(ps[:, 0:n], ws[:, k, j * 128:(j + 1) * 128], self.hT[:, k, c0:c0 + n],
                             start=(k == 0), stop=(k == 7))
                    if samp:
                        P.copy('act', xe[:, j, :, 15], ps[:, 0:n])
                    else:
                        P.copy('act', xe[:, j, 0, 15:15 + n], ps[:, 0:n])
                for j in range(4):
                    ps = self.bank()
                    for k in range(8):
                        P.mm(ps[:, 0:n], ws[:, k, 512 + j * 128:512 + (j + 1) * 128], self.hT[:, k, c0:c0 + n],
                             start=(k == 0), stop=(k == 7))
                    P.act(sz[:, j, 0:n], ps[:, 0:n], AF.Silu)
                if not samp:
                    P.copy('pool', halo[:, g * 4:(g + 1) * 4, :], xe[:, :, 0, T:T + 15])
                src = xe
                bufs = [ta, tb]
                sh = 1
                lvl = 0
                while sh < w:
                    dst = bufs[lvl % 2]
                    lo = 2 * sh - 1
                    P.tt('dve', dst[:, :, :, lo:L], src[:, :, :, lo:L],
                         src[:, :, :, lo - sh:L - sh], ALU.add)
                    src = dst
                    sh *= 2
                    lvl += 1
                if samp:
                    pov = po[:, :, 0:n]
                    P.stt('dve', pov, src[:, :, :, 15], 1.0 / w, xe[:, :, :, 15], ALU.mult, ALU.subtract)
                else:
                    P.stt('dve', po[:, :, 0:n], src[:, :, 0, 15:15 + n], 1.0 / w, xe[:, :, 0, 15:15 + n],
                          ALU.mult, ALU.subtract)
                    if self.p == 0 and ti == 0:
                        for t in range(w - 1):
                            P.stt('pool', po[:, :, t:t + 1], src[:, :, 0, 15 + t:16 + t], 1.0 / (t + 1),
                                  xe[:, :, 0, 15 + t:16 + t], ALU.mult, ALU.subtract)
                def grouped(g=g, c0=c0, n=n, samp=samp, wg=wg, po=po, sz=sz, xe=xe):
                    for j in range(4):
                        ps = self.bank()
                        for k in range(4):
                            P.mm(ps[:, 0:n], wg[:, k, j * 128:(j + 1) * 128], po[:, k, 0:n],
                                 start=(k == 0), stop=(k == 3))
                        c = g * 4 + j
                        P.stt('dve', self.B1[:, c, c0:c0 + n], ps[:, 0:n], scale[:, c:c + 1], sz[:, j, 0:n],
                              ALU.mult, ALU.mult)
                    if samp:
                        ps = self.bank()
                        for j in range(4):
                            P.transpose(ps[0:NS, j * 128:(j + 1) * 128], xe[:, j, :, 15], self.ident)
                        so = self.Rview(hsoff, F32, [512], parts=NS)
                        P.copy('dve', so, ps[0:NS, 0:512])
                        P.dma('sp', st_out_s[:, 14, g * 512:(g + 1) * 512], so)
                if pending is not None:
                    pending()
                pending = grouped
        pending()
        if self.p == 0:
            P.dma('sp', st_out_s[:, 0:14, :], st_in[:, 1:15, :])
        if self.p == self.n_pass - 1:
            ps = self.bank(4)
            for c in range(16):
                P.transpose(ps[0:15, c * 128:(c + 1) * 128], halo[:, c, :], self.ident)
            so = self.Rview(0, F32, [2048], parts=15)
            P.copy('dve', so, ps[0:15, :])
            P.dma('sp', st_out_p, so)
        self.out_proj(w_out, 16, self.B1)

    def out_proj(self, w_out, nk, src):
        P = self.P
        for q in range(4):
            oo = self.wslot()
            wo_ = self.sb.view(oo, BF16, [nk, 256])
            P.dma('pool', wo_, w_out[:, :, q * 256:(q + 1) * 256])
            for (c0, n) in self.tiles:
                for dd in range(2):
                    d = q * 2 + dd
                    ps = self.bank()
                    for k in range(nk):
                        P.mm(ps[:, 0:n], wo_[:, k, dd * 128:(dd + 1) * 128], src[:, k, c0:c0 + n],
                             start=(k == 0), stop=(k == nk - 1))
                    P.tt('dve', self.xT[:, d, c0:c0 + n], self.xT[:, d, c0:c0 + n], ps[:, 0:n], ALU.add)

    def sgu_setup(self):
        P, D, sb = self.P, self.D, self.sb
        self.wsT = sb.new(BF16, [4, 128])
        self.wsTs = sb.new(BF16, [4, NS], parts=NS)
        self.bsr = sb.new(F32, [4, 128], parts=1)
        self.bss = sb.new(F32, [4, NS], parts=1)
        self.bs2 = sb.new(BF16, [512], parts=2)
        self.bss2 = sb.new(BF16, [4, NS], parts=2)
        self.onesf = sb.new(F32, [128])
        self.w00 = sb.new(F32, [4], parts=NS)
        P.memset('pool', self.onesf, 1.0)
        wsn = self.Rview(0, F32, [4, 128])
        P.dma('sp', wsn, D['l1_w_s'].rearrange("g i j -> i g j"))
        for g in range(4):
            P.raw('pool', (lambda g: lambda e: e.affine_select(out=wsn[:, g, :], in_=wsn[:, g, :], pattern=[[-1, 128]],
                                                          compare_op=ALU.is_ge, fill=0.0, base=0,
                                                          channel_multiplier=1))(g),
                  [wsn[:, g, :]], [wsn[:, g, :]])
        ps = self.bank()
        for g in range(4):
            P.transpose(ps[:, g * 128:(g + 1) * 128], wsn[:, g, :], self.ident)
        P.copy('dve', self.wsT, ps.rearrange("p (g i) -> p g i", g=4))
        P.dma('sp', self.bsr, D['l1_b_s'].rearrange("(o g) i -> o g i", o=1))
        P.copy('dve', self.bss, self.bsr[:, :, 0:1].to_broadcast([1, 4, NS]))
        bs2f = self.Rview(4096, F32, [512], parts=2)
        lo_f = self.Rview(8192, F32, [512], parts=2)
        hi_b = self.Rview(12288, BF16, [512], parts=2)
        lo_b = self.Rview(14336, BF16, [512], parts=2)
        P.dma('sp', bs2f, D['l1_b_s'].rearrange("(o g) i -> o (g i)", o=1).to_broadcast([2, 512]))
        P.copy('dve', hi_b, bs2f)
        P.tt('dve', lo_f, bs2f, hi_b, ALU.subtract)
        P.copy('dve', lo_b, lo_f)
        P.copy('dve', self.bs2, hi_b)
        P.dma('sp', self.bs2[1:2, :], lo_b[1:2, :])
        P.copy('dve', self.bss2, self.bs2.rearrange("p (g i) -> p g i", g=4)[:, :, 0:1].to_broadcast([2, 4, NS]))
        for g in range(4):
            P.dma('sp', self.w00[:, g:g + 1], D['l1_w_s'][g, 0:1, 0:1].to_broadcast([NS, 1]))
        for g in range(4):
            P.ts('dve', self.wsTs[:, g, :], self.ident[0:NS, 0:NS], self.w00[:, g:g + 1], None, ALU.mult)

    def sgu_layer(self):
        P, D, O = self.P, self.D, self.O
        w_in = D['l1_w_in'].rearrange("(k p) n -> p k n", p=128)
        w_out = D['l1_w_out'].rearrange("(k p) n -> p k n", p=128)
        gbc = self.Rview(0, F32, [2048])
        bbc = self.Rview(8192, F32, [2048])
        vraw = self.Rview(16384, F32, [2048])
        vn = self.Rview(24576, BF16, [2048])
        uz = self.Rview(28672, F32, [4, 512])
        uu = self.Rview(36864, F32, [4, 512])
        st = self.Rview(45056, F32, [8])
        P.dma('sp', gbc, D['l1_ln_g'].rearrange("(o n) -> o n", o=1).to_broadcast([128, 2048]))
        P.dma('sp', bbc, D['l1_ln_b'].rearrange("(o n) -> o n", o=1).to_broadcast([128, 2048]))
        wv = []
        for hh in range(2):
            o = self.wslot()
            v = self.sb.view(o, BF16, [8, 1024])
            P.dma('pool', v[:, :, 0:512], w_in[:, :, 2048 + hh * 1024:2048 + hh * 1024 + 512])
            P.dma('pool', v[:, :, 512:1024], w_in[:, :, 2048 + hh * 1024 + 512:2048 + (hh + 1) * 1024])
            wv.append(v)
        chunks = [(tc * 128, 128) for tc in range(8)] + ([(1024, NS)] if self.p == 0 else [])
        vr2 = self.Rview(45120, F32, [2048])
        vn2 = self.Rview(53312, BF16, [2048])
        st2 = self.Rview(57408, F32, [8])
        vbufs = [(vraw, vn, st), (vr2, vn2, st2)]
        def vmm(ci):
            c0, n = chunks[ci]
            self.ensure_h(min(c0 // 512, 2))
            ps4 = self.psum[:, (ci % 2) * 2048:(ci % 2 + 1) * 2048]
            for s_ in range(4):
                for k in range(8):
                    P.mm(ps4[0:n, s_ * 512:(s_ + 1) * 512], self.hT[:, k, c0:c0 + n],
                         wv[s_ // 2][:, k, (s_ % 2) * 512:(s_ % 2 + 1) * 512], start=(k == 0), stop=(k == 7))

        def rest(ci):
            c0, n = chunks[ci]
            vraw, vn, st = vbufs[ci % 2]
            samp = (n == NS)
            ps4 = self.psum[:, (ci % 2) * 2048:(ci % 2 + 1) * 2048]
            P.act(vraw[0:n, :], ps4[0:n, :], AF.Gelu, accum_out=st[0:n, 0:1])
            P.ts('dve', st[0:n, 2:3], st[0:n, 0:1], -1.0 / 2048, None, ALU.mult)
            P.ts('dve', vraw[0:n, :], vraw[0:n, :], st[0:n, 2:3], None, ALU.add)
            P.act(vn[0:n, :], vraw[0:n, :], AF.Square, accum_out=st[0:n, 1:2])
            P.act(st[0:n, 3:4], st[0:n, 1:2], AF.Sqrt, scale=1.0 / 2048, bias=self.epsc[0:n, 0:1])
            P.recip(st[0:n, 3:4], st[0:n, 3:4])
            P.stt('dve', vraw[0:n, :], vraw[0:n, :], st[0:n, 3:4], gbc[0:n, :], ALU.mult, ALU.mult)
            if samp:
                P.tt('dve', vraw[0:n, :], vraw[0:n, :], bbc[0:n, :], ALU.add)
                P.copy('dve', vn[0:n, :], vraw[0:n, :])
                P.dma('sp', O['v1s'], vraw[0:n, :])
            else:
                P.tt('dve', vn[0:n, :], vraw[0:n, :], bbc[0:n, :], ALU.add)
            pss = ps4
            for cc in range(16):
                g = cc // 4
                o_ = pss[:, cc * n:(cc + 1) * n]
                if samp:
                    P.mm(o_, vn[0:n, cc * 128:(cc + 1) * 128], self.wsTs[:, g, :], start=True, stop=False)
                    P.mm(o_, self.onesb[0:2, :], self.bss2[:, g, :], start=False, stop=True)
                else:
                    P.mm(o_, vn[:, cc * 128:(cc + 1) * 128], self.wsT[:, g, :], start=True, stop=False)
                    P.mm(o_, self.onesb[0:2, :], self.bs2[:, g * 128:(g + 1) * 128], start=False, stop=True)
            if samp:
                P.copy('act', self.B1[:, :, c0:c0 + n], pss[:, 0:16 * n].rearrange("p (c i) -> p c i", c=16))
            else:
                for b in range(4):
                    P.copy('act' if b % 2 == 0 else 'dve', self.B1[:, 4 * b:4 * b + 4, c0:c0 + n],
                           pss[:, b * 512:(b + 1) * 512].rearrange("p (c i) -> p c i", c=4))

        vmm(0)
        for ci in range(len(chunks)):
            if ci + 1 < len(chunks):
                vmm(ci + 1)
            rest(ci)
        for g in range(4):
            o = self.wslot()
            wz = self.sb.view(o, BF16, [8, 1024])
            P.dma('pool', wz[:, :, 0:512], w_in[:, :, g * 512:(g + 1) * 512])
            P.dma('pool', wz[:, :, 512:1024], w_in[:, :, 4096 + g * 512:4096 + (g + 1) * 512])
            for ti, (c0, n) in enumerate(self.tiles):
                if (g * len(self.tiles) + ti) % 2:
                    uz = self.Rview(16384, F32, [4, 512])
                    uu = self.Rview(45120, F32, [4, 512])
                else:
                    uz = self.Rview(28672, F32, [4, 512])
                    uu = self.Rview(36864, F32, [4, 512])
                for j in range(4):
                    ps = self.bank()
                    for k in range(8):
                        P.mm(ps[:, 0:n], wz[:, k, j * 128:(j + 1) * 128], self.hT[:, k, c0:c0 + n],
                             start=(k == 0), stop=(k == 7))
                    P.act(uu[:, j, 0:n], ps[:, 0:n], AF.Gelu)
                for j in range(4):
                    ps = self.bank()
                    for k in range(8):
                        P.mm(ps[:, 0:n], wz[:, k, 512 + j * 128:512 + (j + 1) * 128], self.hT[:, k, c0:c0 + n],
                             start=(k == 0), stop=(k == 7))
                    P.act(uz[:, j, 0:n], ps[:, 0:n], AF.Silu)
                P.tt('dve', uz[:, :, 0:n], uz[:, :, 0:n], uu[:, :, 0:n], ALU.mult)
                P.tt('dve', self.B1[:, g * 4:(g + 1) * 4, c0:c0 + n], self.B1[:, g * 4:(g + 1) * 4, c0:c0 + n],
                     uz[:, :, 0:n], ALU.mult)
        self.out_proj(w_out, 16, self.B1)

    def dn_setup(self):
        P, D, sb = self.P, self.D, self.sb
        self.S = sb.new(F32, [8, 128])
        self.haloc = sb.new(F32, [24, 3])
        self.convw = sb.new(F32, [96])
        self.ogain = sb.new(F32, [1])
        self.dtb = sb.new(F32, [1], parts=8)
        self.nA = sb.new(F32, [1], parts=8)
        self.load_pm(D['l2_conv_w'].rearrange("j c -> (j c)"), 4 * 3072, self.convw)
        P.dma('sp', self.ogain, D['l2_o_gain'].rearrange("(p o) -> p o", o=1))
        P.dma('sp', self.dtb, D['l2_dt_bias'].rearrange("(p o) -> p o", o=1))
        P.dma('sp', self.nA, D['l2_a_log'].rearrange("(p o) -> p o", o=1))
        P.act(self.nA, self.nA, AF.Exp)
        P.ts('dve', self.nA, self.nA, -1.0, None, ALU.mult)
        P.memset('pool', self.S, 0.0)
        print("SBUF bytes after dn_setup:", sb.off)

    def dn_layer(self):
        P, D, O = self.P, self.D, self.O
        w_in = D['l2_w_in'].rearrange("(k p) n -> p k n", p=128)
        w_out = D['l2_w_out'].rearrange("(k p) n -> p k n", p=128)
        qT = self.B1[:, 0:8, :]
        kT = self.B1[:, 8:16, :]
        vT = self.Rview(0, BF16, [8, NCOL])
        sz = self.Rview(16640, BF16, [8, NCOL])
        WK = 33280
        G = self.Rview(WK + 19840, F32, [NCOL], parts=8)
        LB = self.Rview(WK + 19840 + 4160, F32, [NCOL], parts=8)
        accoff = WK + 8256
        sq = self.Rview(WK + 16448, BF16, [512])
        rs = self.Rview(WK + 17472, F32, [512])
        W0 = self.Woff[0] - self.Roff
        wl = [W0, W0 + 8192, W0 + 16384]
        xcoffs = [WK, W0 + 24576]
        accoffs = [WK + 8256, W0 + 32832]
        sq4 = self.Rview(W0 + 41024, BF16, [4, 512])
        unit = 0
        for s_ in range(8):
            ws = self.Rview(wl[s_ % 3], BF16, [8, 512])
            P.dma('pool', ws, w_in[:, :, s_ * 512:(s_ + 1) * 512])
            kind = 'qkvz'[s_ // 2]
            for ti, (c0, n) in enumerate(self.tiles):
                self.ensure_h(ti)
                samp = (n == NS)
                if samp:
                    nseq, T = NS, 1
                else:
                    nseq, T = 1, n
                L = 3 + T
                if kind != 'z':
                    unit += 1
                xcoff, accoff = xcoffs[unit % 2], accoffs[unit % 2]
                xc = self.Rview(xcoff, F32, [4, nseq, L])
                acc = self.Rview(accoff, F32, [4, 512])
                if kind != 'z':
                    if samp:
                        hs = self.Rview(accoff, F32, [512], parts=48)
                        P.dma('sp', hs, D['sc2'][:, :, s_ * 512:(s_ + 1) * 512].rearrange("b r c -> (b r) c"))
                        ps = self.bank()
                        for j in range(4):
                            P.transpose(ps[:, j * 48:(j + 1) * 48], hs[:, j * 128:(j + 1) * 128],
                                        self.ident[0:48, 0:48])
                        P.copy('dve', xc[:, :, :, 0:3], ps[:, 0:192].rearrange("p (j b r) -> p j b r", j=4, b=NS))
                    elif self.p == 0 and ti == 0:
                        P.memset('pool', xc[:, :, :, 0:3], 0.0)
                    else:
                        P.copy('pool', xc[:, :, 0, 0:3], self.haloc[:, s_ * 4:(s_ + 1) * 4, :])
                for j in range(4):
                    ps = self.bank()
                    for k in range(8):
                        P.mm(ps[:, 0:n], ws[:, k, j * 128:(j + 1) * 128], self.hT[:, k, c0:c0 + n],
                             start=(k == 0), stop=(k == 7))
                    if kind == 'z':
                        P.act(sz[:, (s_ - 6) * 4 + j, c0:c0 + n], ps[:, 0:n], AF.Silu)
                    elif samp:
                        P.copy('act', xc[:, j, :, 3], ps[:, 0:n])
                    else:
                        P.copy('act', xc[:, j, 0, 3:3 + n], ps[:, 0:n])
                if kind == 'z':
                    continue
                if not samp:
                    P.copy('pool', self.haloc[:, s_ * 4:(s_ + 1) * 4, :], xc[:, :, 0, T:T + 3])
                else:
                    ps = self.bank()
                    for j in range(4):
                        P.transpose(ps[0:NS, j * 128:(j + 1) * 128], xc[:, j, :, 3], self.ident)
                    so = self.Rview(WK + 17472, F32, [512], parts=NS)
                    P.copy('dve', so, ps[0:NS, 0:512])
                    P.dma('sp', O['c2s'][:, 2, s_ * 512:(s_ + 1) * 512], so)
                def xv(j, tap):
                    return xc[:, j, :, tap] if samp else xc[:, j, 0, tap:tap + n]
                for tap in range(4):
                    for j in range(4):
                        ch = s_ * 4 + j
                        a_ = acc[:, j, 0:n]
                        if tap == 0:
                            P.ts('dve', a_, xv(j, 0), self.convw[:, ch:ch + 1], None, ALU.mult)
                        else:
                            P.stt('dve', a_, xv(j, tap), self.convw[:, tap * 24 + ch:tap * 24 + ch + 1], a_,
                                  ALU.mult, ALU.add)
                if kind == 'v':
                    P.act(vT[:, (s_ - 4) * 4:(s_ - 4) * 4 + 4, c0:c0 + n], acc[:, :, 0:n], AF.Silu)
                    continue
                P.act(acc[:, :, 0:n], acc[:, :, 0:n], AF.Silu)
                rs4 = self.Rview(xcoff, F32, [4, 512])
                P.act(sq4[:, :, 0:n], acc[:, :, 0:n], AF.Square)
                ps4 = self.bank(4)
                for j in range(4):
                    P.mm(ps4[:, j * 512:j * 512 + n], self.onesb, sq4[:, j, 0:n])
                P.act(rs4[:, :, 0:n], ps4.rearrange("p (j t) -> p j t", j=4)[:, :, 0:n], AF.Ln, bias=self.epsc[:, 0:1])
                P.act(rs4[:, :, 0:n], rs4[:, :, 0:n], AF.Exp, scale=-0.5)
                hd0 = (s_ % 2) * 4
                if kind == 'q':
                    P.stt('dve', qT[:, hd0:hd0 + 4, c0:c0 + n], acc[:, :, 0:n], 128.0 ** -0.5, rs4[:, :, 0:n],
                          ALU.mult, ALU.mult)
                else:
                    P.tt('dve', kT[:, hd0:hd0 + 4, c0:c0 + n], acc[:, :, 0:n], rs4[:, :, 0:n], ALU.mult)
        o = self.wslot()
        wab = self.sb.view(o, BF16, [8, 16])
        P.dma('pool', wab, w_in[:, :, 4096:4112])
        for (c0, n) in self.tiles:
            ps = self.bank()
            for k in range(8):
                P.mm(ps[0:8, 0:n], wab[:, k, 0:8], self.hT[:, k, c0:c0 + n], start=(k == 0), stop=(k == 7))
            P.act(G[:, c0:c0 + n], ps[0:8, 0:n], AF.Exp, bias=self.dtb[:, 0:1])
            P.act(G[:, c0:c0 + n], G[:, c0:c0 + n], AF.Ln, bias=1.0)
            P.ts('dve', G[:, c0:c0 + n], G[:, c0:c0 + n], self.nA[:, 0:1], None, ALU.mult)
            ps = self.bank()
            for k in range(8):
                P.mm(ps[0:8, 0:n], wab[:, k, 8:16], self.hT[:, k, c0:c0 + n], start=(k == 0), stop=(k == 7))
            P.act(LB[:, c0:c0 + n], ps[0:8, 0:n], AF.Exp, scale=-1.0)
            P.act(LB[:, c0:c0 + n], LB[:, c0:c0 + n], AF.Ln, bias=1.0)
            P.ts('dve', LB[:, c0:c0 + n], LB[:, c0:c0 + n], -1.0, None, ALU.mult)
        if self.p == 0:
            P.dma('sp', O['c2s'][:, 0:2, :], D['sc2'][:, 1:3, :])
        if self.p == self.n_pass - 1:
            ps = self.bank(6)
            for c in range(24):
                P.transpose(ps[0:3, c * 128:(c + 1) * 128], self.haloc[:, c, :], self.ident)
            so = self.Rview(WK, F32, [3072], parts=3)
            P.copy('dve', so, ps[0:3, :])
            P.dma('sp', O['c2p'], so)

        dn_stop = 0
        om = self.hT
        W1 = self.Woff[1] - self.Roff

        def Wv(off, dt, shape, parts=128, p0=0):
            return self.Rview(W1 + off, dt, shape, parts, p0)
        Esel = self.Rview(WK + 0, F32, [8, 128], parts=8)
        Ucum = self.Rview(WK + 4096, F32, [128])
        Ublk = self.Rview(WK + 4608, F32, [128])
        Ua = self.Rview(WK + 5120, F32, [128])
        Ub = self.Rview(WK + 5632, F32, [128])
        maskS = self.Rview(WK + 6144, F32, [128])
        maskI = self.Rview(WK + 6656, F32, [128])
        colA = self.Rview(WK + 7168, F32, [16])
        ctmp = self.Rview(WK + 7232, F32, [40])
        EX = self.Rview(WK + 7392, F32, [40])
        GAMr = self.Rview(WK + 7552, F32, [128], parts=8)
        GLr = self.Rview(WK + 8064, F32, [128], parts=8)
        EGr = self.Rview(WK + 8576, F32, [128], parts=8)
        XA = self.Rview(WK + 9216, F32, [8, 128])
        Sb = self.Rview(WK + 13312, BF16, [8, 128])
        w_sb = self.Rview(WK + 15360, BF16, [8, 128])
        osq = self.Rview(WK + 17408, BF16, [8, 128])
        Abuf = [Wv(0, BF16, [4, 128]), Wv(1024, BF16, [4, 128])]
        Bbuf = [Wv(2048, BF16, [4, 128]), Wv(3072, BF16, [4, 128])]
        Tbuf = [Wv(4096, BF16, [4, 128]), Wv(5120, BF16, [4, 128])]
        TTf = Wv(6144, BF16, [8, 128])
        qk = Wv(8192, BF16, [8, 128])
        qkT = Wv(10240, BF16, [8, 128])
        kb = Wv(12288, BF16, [8, 128])
        kt = Wv(14336, BF16, [8, 128])
        bv = Wv(16384, BF16, [8, 128])
        nwkT = Wv(18432, BF16, [8, 128])
        qd = Wv(20480, BF16, [8, 128])
        tmpf = Wv(22528, F32, [8, 128])
        BIG = -30000.0
        P.memset('pool', Esel, 0.0)
        for h in range(8):
            P.copy('pool', Esel[:, h, :], self.ident[0:8, h:h + 1].to_broadcast([8, 128]))
        P.memset('pool', Ucum, 1.0)
        P.raw('pool', lambda e: e.affine_select(out=Ucum, in_=Ucum, pattern=[[1, 128]], compare_op=ALU.is_ge,
                                               fill=0.0, base=0, channel_multiplier=-1), [Ucum], [Ucum])
        P.memset('pool', Ucum[0:64, 64:128], 0.0)
        P.memset('pool', Ublk, 0.0)
        P.memset('pool', Ublk[0:64, 0:64], 1.0)
        P.memset('pool', Ublk[64:128, 64:128], 1.0)
        P.memset('pool', Ua, 0.0)
        P.memset('pool', Ua[0:64, :], 1.0)
        P.memset('pool', Ub, 0.0)
        P.memset('pool', Ub[64:128, :], 1.0)
        for m_, op_ in ((maskS, ALU.is_gt), (maskI, ALU.is_ge)):
            P.memset('pool', m_, 0.0)
            P.raw('pool', (lambda m_, op_: lambda e: e.affine_select(out=m_, in_=m_, pattern=[[-1, 128]],
                                                                   compare_op=op_, fill=BIG, base=0,
                                                                   channel_multiplier=1))(m_, op_), [m_], [m_])
            P.memset('pool', m_[64:128, 0:64], BIG)
        P.copy('act', Sb, self.S)
        self.bank_n = 6
        self.pbank = 0

        def finish(o_ps, c0, n):
            ov = o_ps.rearrange("p (h i) -> p h i", h=8)
            P.act(osq[:, :, 0:n], ov, AF.Square)
            ps = self.bank(2) if n == 128 else self.bank(1)
            for h in range(8):
                P.mm(ps[:, h * n:(h + 1) * n], self.onesb, osq[:, h, 0:n])
            psv = ps[:, 0:8 * n].rearrange("p (h i) -> p h i", h=8)
            P.rsqrt(XA[:, :, 0:n], psv, 1.0 / 128, self.epsc[:, 0:1])
            P.tt('dve', XA[:, :, 0:n], XA[:, :, 0:n], sz[:, :, c0:c0 + n], ALU.mult)
            P.stt('dve', om[:, :, c0:c0 + n], ov, self.ogain[:, 0:1], XA[:, :, 0:n], ALU.mult, ALU.mult)

        EXs = [EX, Wv(30720, F32, [40])]
        kts = [kt, self.Rview(W0 + 12288, BF16, [8, 128])]
        bvs = [bv, self.Rview(W0 + 10240, BF16, [8, 128])]
        gamc = self.Rview(WK + 9088, F32, [8])

        def front(d):
            c0 = d * 128
            cs = slice(c0, c0 + 128)
            EXd, ktd, bvd = EXs[d % 2], kts[d % 2], bvs[d % 2]
            psA = self.bank()
            P.transpose(psA[:, 0:8], G[:, cs], self.ident[0:8, 0:8])
            P.transpose(psA[:, 8:16], LB[:, cs], self.ident[0:8, 0:8])
            P.copy('dve', colA, psA[:, 0:16])
            gT, lbT = colA[:, 0:8], colA[:, 8:16]
            psB = self.bank()
            P.mm(psB[0:8, 0:128], gT, Ucum)
            psC = self.bank()
            P.mm(psC[:, 0:8], Ucum, gT)
            P.mm(psC[:, 8:16], Ublk, gT)
            P.mm(psC[:, 16:24], Ua, gT)
            P.mm(psC[:, 24:32], Ub, gT)
            P.act(GAMr, psB[0:8, 0:128], AF.Copy, scale=-1.0)
            P.copy('dve', ctmp[:, 0:8], lbT)
            P.tt('dve', ctmp[:, 8:16], psC[:, 0:8], lbT, ALU.add)
            gamc = self.Rview(WK + 9088, F32, [8])
            P.copy('dve', gamc, psC[:, 0:8])
            P.tt('dve', ctmp[:, 16:24], psC[:, 8:16], gamc, ALU.subtract)
            P.copy('dve', ctmp[:, 24:40], psC[:, 16:32])
            P.act(EXd, ctmp, AF.Exp)
            beta_c, cb_c, ct_c = EXd[:, 0:8], EXd[:, 8:16], EXd[:, 16:24]
            al_c = [EXd[:, 24:32], EXd[:, 32:40]]

            def bcol(col):
                return col.unsqueeze(2).to_broadcast([128, 8, 128])
            psk = self.bank(1).bitcast(BF16)
            for h in range(8):
                P.transpose(psk[:, h * 128:(h + 1) * 128], kT[:, h, cs], self.identb)
            pskv = psk.rearrange("p (h d) -> p h d", h=8)
            P.tt('dve', kb, pskv, bcol(cb_c), ALU.mult)
            P.tt('dve', ktd, pskv, bcol(ct_c), ALU.mult)
            psv_ = self.bank(1).bitcast(BF16)
            for h in range(8):
                P.transpose(psv_[:, h * 128:(h + 1) * 128], vT[:, h, cs], self.identb)
            P.tt('dve', bvd, psv_.rearrange("p (h d) -> p h d", h=8), bcol(beta_c), ALU.mult)
            return dict(al_c=al_c, kt=ktd, bv=bvd)

        phase = 99
        for dc in range(8):
            c0 = dc * 128
            cs = slice(c0, c0 + 128)
            if dc == 0:
                fr = front(0)
            al_c, kt, bv = fr['al_c'], fr['kt'], fr['bv']
            if phase < 3:
                continue
            BDg = self.Rview(W0 + 6144, F32, [8, 128], parts=8)
            BDe = self.Rview(W0 + 10240, F32, [8, 128], parts=8)
            P.tt('pool', BDg, Esel, GAMr.unsqueeze(1).to_broadcast([8, 8, 128]), ALU.mult)
            Eflat = Esel.rearrange("p h j -> p (h j)")
            BDgf = BDg.rearrange("p h j -> p (h j)")
            BDef = BDe.rearrange("p h j -> p (h j)")
            ones8 = self.onesf[0:8, :]
            psD = self.bank(2)
            for hf in range(2):
                fs = slice(hf * 512, (hf + 1) * 512)
                P.mm(psD[:, fs], ones8, BDgf[:, fs])
            psDv = psD.rearrange("p (h j) -> p h j", h=8)
            sub = 99
            if sub == 0:
                P.copy('dve', XA, psDv)
                continue
            pskk = self.bank(2)
            for h in range(8):
                P.mm(pskk[:, h * 128:(h + 1) * 128], kT[:, h, cs], kT[:, h, cs])
            psqk = self.bank(2)
            for h in range(8):
                P.mm(psqk[:, h * 128:(h + 1) * 128], qT[:, h, cs], kT[:, h, cs])
            mS = maskS.unsqueeze(1).to_broadcast([128, 8, 128])
            mI = maskI.unsqueeze(1).to_broadcast([128, 8, 128])
            egbc = tmpf
            P.act(egbc, psDv, AF.Exp, scale=-1.0)
            P.tt('dve', XA, psDv, mS, ALU.add)
            P.tt('dve', XA, XA, ctmp[:, 8:16].unsqueeze(2).to_broadcast([128, 8, 128]), ALU.add)
            P.act(XA, XA, AF.Exp)
            if sub == 1:
                continue
            A0 = Wv(26624, BF16, [8, 128])
            B0 = Wv(28672, BF16, [8, 128])
            P.stt('dve', A0, pskk.rearrange("p (h j) -> p h j", h=8), -1.0, XA, ALU.mult, ALU.mult)
            if sub == 2:
                continue
            P.tt('dve', XA, psDv, gamc.unsqueeze(2).to_broadcast([128, 8, 128]), ALU.add)
            if sub == 3:
                continue
            P.tt('dve', XA, XA, mI, ALU.add)
            if sub == 4:
                continue
            P.act(XA, XA, AF.Exp)
            if sub == 5:
                continue
            if sub == 6:
                P.copy('dve', XA, psqk.rearrange("p (h j) -> p h j", h=8))
                continue
            if sub == 7:
                P.copy('dve', qk, XA)
                continue
            if sub == 8:
                P.copy('act', qk, XA)
                continue
            if sub == 9:
                P.copy('dve', qk[:, :, 0:16], colA.unsqueeze(1).to_broadcast([128, 8, 16]))
                continue
            if sub == 10:
                P.copy('dve', kb, XA)
                continue
            P.tt('dve', qk, psqk.rearrange("p (h j) -> p h j", h=8), XA, ALU.mult)
            if phase < 4:
                continue
            pst = self.bank(1).bitcast(BF16)
            for h in range(8):
                P.transpose(pst[:, h * 128:(h + 1) * 128], A0[:, h, :], self.identb)
            pstv = pst.rearrange("p (h j) -> p h j", h=8)
            P.copy('act', B0, pstv)
            pst2 = self.bank(1).bitcast(BF16)
            for h in range(8):
                P.transpose(pst2[:, h * 128:(h + 1) * 128], qk[:, h, :], self.identb)
            P.copy('act', qkT, pst2.rearrange("p (h j) -> p h j", h=8))
            if phase < 5:
                continue
            idb = self.identb.unsqueeze(1).to_broadcast([128, 4, 128])
            st_ = []
            for hg in range(2):
                hs_ = slice(hg * 4, hg * 4 + 4)
                if hg == 0:
                    bufs = (Abuf, Bbuf, Tbuf)
                else:
                    bufs = ([self.Rview(W0 + 0, BF16, [4, 128]), self.Rview(W0 + 1024, BF16, [4, 128])],
                            [self.Rview(W0 + 2048, BF16, [4, 128]), self.Rview(W0 + 3072, BF16, [4, 128])],
                            [self.Rview(W0 + 4096, BF16, [4, 128]), self.Rview(W0 + 5120, BF16, [4, 128])])
                P.tt('dve', bufs[2][0], B0[:, hs_, :], idb, ALU.add)
                st_.append(dict(Ap=A0[:, hs_, :], Bp=B0[:, hs_, :], Tp=bufs[2][0], bufs=bufs, hs=hs_))
            for r in range(1, 7):
                for hg in range(2):
                    d_ = st_[hg]
                    Ap, Bp, Tp = d_['Ap'], d_['Bp'], d_['Tp']
                    An, Bn, Tn = d_['bufs'][0][r % 2], d_['bufs'][1][r % 2], d_['bufs'][2][r % 2]
                    if r <= 5:
                        pa = self.bank()
                        for h in range(4):
                            P.mm(pa[:, h * 128:(h + 1) * 128], Bp[:, h, :], Ap[:, h, :])
                    if r <= 4:
                        pb = self.bank()
                        for h in range(4):
                            P.mm(pb[:, h * 128:(h + 1) * 128], Ap[:, h, :], Bp[:, h, :])
                    if r >= 2:
                        pt = self.bank()
                        for h in range(4):
                            P.mm(pt[:, h * 128:(h + 1) * 128], Ap[:, h, :], Tp[:, h, :])
                    if r <= 5:
                        P.copy('act', An, pa.rearrange("p (h j) -> p h j", h=4))
                        d_['Ap'] = An
                    if r <= 4:
                        P.copy('act', Bn, pb.rearrange("p (h j) -> p h j", h=4))
                        d_['Bp'] = Bn
                    if r >= 2:
                        dst = Tn if r < 6 else TTf[:, d_['hs'], :]
                        P.tt('dve', dst, pt.rearrange("p (h j) -> p h j", h=4), Tp, ALU.add)
                        d_['Tp'] = dst
            if phase < 6:
                continue
            pswk = self.bank(2)
            for h in range(8):
                P.mm(pswk[:, h * 128:(h + 1) * 128], kb[:, h, :], TTf[:, h, :])
            P.act(nwkT, pswk.rearrange("p (h i) -> p h i", h=8), AF.Copy, scale=-1.0)
            P.tt('dve', qd, qT[:, :, cs], egbc, ALU.mult)
            if phase < 7:
                continue
            pso = self.psum[:, 6 * 512:8 * 512]
            for x in range(2):
                r_ = slice(64 * x, 64 * x + 64)
                psw = self.bank(2)
                for h in range(8):
                    P.mm(psw[r_, h * 128:(h + 1) * 128], TTf[r_, h, r_], bv[r_, h, :], start=True, stop=False)
                    P.mm(psw[r_, h * 128:(h + 1) * 128], nwkT[:, h, r_], Sb[:, h, :], start=False, stop=True)
                P.copy('act', w_sb[r_, :, :], psw[r_, :].rearrange("p (h d) -> p h d", h=8))
                for h in range(8):
                    oc = pso[:, h * 128 + 64 * x:h * 128 + 64 * x + 64]
                    P.mm(oc, Sb[:, h, :], qd[:, h, r_], start=True, stop=False)
                    P.mm(oc, w_sb[r_, h, :], qkT[r_, h, r_], start=False, stop=True)
                P.tt('dve', self.S, self.S, al_c[x].unsqueeze(2).to_broadcast([128, 8, 128]), ALU.mult)
                pss_ = self.bank(2)
                for h in range(8):
                    P.mm(pss_[:, h * 128:(h + 1) * 128], kt[r_, h, :], w_sb[r_, h, :])
                P.tt('dve', Sb, self.S, pss_.rearrange("p (h d) -> p h d", h=8), ALU.add)
                P.tt('dve', self.S, self.S, pss_.rearrange("p (h d) -> p h d", h=8), ALU.add)
                if x == 0 and dc + 1 < 8:
                    fr_next = front(dc + 1)
            finish(pso, c0, 128)
            if dc + 1 < 8:
                fr = fr_next
        if self.p == self.n_pass - 1:
            P.dma('sp', O['d2p'].rearrange("h k v -> k h v"), self.S)
        if self.p == 0 and dn_stop != 3:
            self.dn_samples(finish, Esel, G, LB, qT, kT, vT, Wv)
        self.bank_n = 8
        self.pbank = 0
        self.out_proj(w_out, 8, om)

    def dn_samples(self, finish, Esel, G, LB, qT, kT, vT, Wv):
        P, D, O = self.P, self.D, self.O
        sc = slice(1024, 1024 + NS)
        W0 = self.Woff[0] - self.Roff
        d16 = Wv(0, BF16, [NS, NS])
        kmask = Wv(512, BF16, [8, NS, NS])
        qmask = Wv(4608, BF16, [8, NS, NS])
        ab_tm = Wv(8704, F32, [16], parts=NS)
        v_tm = Wv(8768, F32, [8, 128], parts=NS)
        r_tm = Wv(12864, BF16, [8, 128], parts=NS)
        k_tm = Wv(14912, BF16, [8, 128], parts=NS)
        rmk = Wv(16960, BF16, [8, 128], parts=NS)
        t_tm = Wv(19008, F32, [8, 128], parts=NS)
        Sb0 = [Wv(23104, BF16, [8, 128]), Wv(25152, BF16, [8, 128])]
        Snb = Wv(27200, BF16, [8, 128])
        abc = Wv(29248, F32, [8, NS])
        S0 = [self.Rview(W0 + 0, F32, [8, 128]), self.Rview(W0 + 4096, F32, [8, 128])]
        Snew = [self.Rview(W0 + 8192, F32, [8, 128]), self.Rview(W0 + 12288, F32, [8, 128])]
        self.bank_n = 4
        self.pbank = 0
        pacc = self.psum[:, 4 * 512:6 * 512]
        pso = self.psum[:, 6 * 512:7 * 512]
        P.memset('pool', d16, 1.0)
        P.raw('pool', lambda e: e.affine_select(out=d16, in_=d16, pattern=[[1, NS], [-1, NS]],
                                               compare_op=ALU.is_equal, fill=0.0, base=0, channel_multiplier=0),
              [d16], [d16])
        d16b = d16.unsqueeze(1).to_broadcast([128, 8, NS, NS])
        P.tt('dve', kmask, kT[:, :, sc].unsqueeze(2).to_broadcast([128, 8, NS, NS]), d16b, ALU.mult)
        P.tt('dve', qmask, qT[:, :, sc].unsqueeze(2).to_broadcast([128, 8, NS, NS]), d16b, ALU.mult)
        ps = self.bank()
        P.transpose(ps[0:NS, 0:8], G[:, sc], self.ident[0:8, 0:8])
        P.transpose(ps[0:NS, 8:16], LB[:, sc], self.ident[0:8, 0:8])
        P.act(ab_tm, ps[0:NS, 0:16], AF.Exp)
        ps = self.bank()
        for h in range(8):
            P.mm(ps[:, h * NS:(h + 1) * NS], Esel[:, h, :], G[:, sc])
        P.act(abc, ps[:, 0:128].rearrange("p (h b) -> p h b", h=8), AF.Exp)
        pk = self.bank(1).bitcast(BF16)
        for h in range(8):
            P.transpose(pk[0:NS, h * 128:(h + 1) * 128], kT[:, h, sc], self.identb)
        P.copy('dve', k_tm, pk[0:NS, :].rearrange("p (h d) -> p h d", h=8))
        pv = self.bank(1).bitcast(BF16)
        for h in range(8):
            P.transpose(pv[0:NS, h * 128:(h + 1) * 128], vT[:, h, sc], self.identb)
        P.copy('dve', v_tm, pv[0:NS, :].rearrange("p (h d) -> p h d", h=8))
        paccv = pacc[0:NS, :].rearrange("p (h d) -> p h d", h=8)
        P.memset('dve', pacc[0:NS, :], 0.0)
        for b in range(NS):
            P.dma('sp', S0[b % 2], D['sd2'][b].rearrange("h k v -> k h v"))
            P.copy('act', Sb0[b % 2], S0[b % 2])
            for h in range(8):
                self.P.add('pe', (lambda o_, l_, r_: lambda e: e.matmul(o_, l_, r_, start=False, stop=False,
                                                                      skip_group_check=True))(
                    pacc[0:NS, h * 128:(h + 1) * 128], kmask[:, h, b, :], Sb0[b % 2][:, h, :]),
                    [kmask[:, h, b, :], Sb0[b % 2][:, h, :]], [pacc[0:NS, h * 128:(h + 1) * 128]])
        P.tt('dve', t_tm, paccv, ab_tm[:, 0:8].unsqueeze(2).to_broadcast([NS, 8, 128]), ALU.mult)
        P.tt('dve', t_tm, v_tm, t_tm, ALU.subtract)
        P.tt('dve', r_tm, t_tm, ab_tm[:, 8:16].unsqueeze(2).to_broadcast([NS, 8, 128]), ALU.mult)
        P.memset('dve', pacc[0:NS, :], 0.0)
        for b in range(NS):
            P.dma('sp', S0[b % 2], D['sd2'][b].rearrange("h k v -> k h v"))
            P.ts('dve', rmk, r_tm, self.ident[0:NS, b:b + 1], None, ALU.mult)
            pss_ = self.bank(2)
            for h in range(8):
                P.mm(pss_[:, h * 128:(h + 1) * 128], k_tm[:, h, :], rmk[:, h, :])
            Sn = Snew[b % 2]
            P.tt('dve', Sn, S0[b % 2], abc[:, :, b].unsqueeze(2).to_broadcast([128, 8, 128]), ALU.mult)
            P.tt('dve', Sn, Sn, pss_.rearrange("p (h d) -> p h d", h=8), ALU.add)
            P.dma('sp', O['d2s'][b].rearrange("h k v -> k h v"), Sn)
            P.copy('act', Snb, Sn)
            for h in range(8):
                self.P.add('pe', (lambda o_, l_, r_: lambda e: e.matmul(o_, l_, r_, start=False, stop=False,
                                                                      skip_group_check=True))(
                    pacc[0:NS, h * 128:(h + 1) * 128], qmask[:, h, b, :], Snb[:, h, :]),
                    [qmask[:, h, b, :], Snb[:, h, :]], [pacc[0:NS, h * 128:(h + 1) * 128]])
        P.copy('dve', t_tm, paccv)
        for h in range(8):
            P.transpose(pso[:, h * NS:(h + 1) * NS], t_tm[:, h, :], self.ident[0:NS, 0:NS])
        finish(pso[:, 0:8 * NS], 1024, NS)

    def final_out(self, p):
        P, O = self.P, self.O
        ss = self.Rview(16384, F32, [2])
        junk = self.Rview(20480, F32, [1024])
        self.fgain_bc = self.Rview(24576, F32, [1024])
        P.dma('sp', self.fgain_bc, self.D['final_norm'].rearrange("(o n) -> o n", o=1).to_broadcast([128, 1024]))
        for tc in range(8 + (1 if p == 0 else 0)):
            if tc < 8:
                c0, n = tc * 128, 128
                dst = O['yp'][p * PASS + tc * 128: p * PASS + (tc + 1) * 128, :]
            else:
                c0, n = 1024, NS
                dst = O['ys']
            ps = self.bank(2)
            for k in range(8):
                P.transpose(ps[0:n, k * 128:(k + 1) * 128], self.xT[:, k, c0:c0 + n], self.ident)
            P.act(junk[0:n, :], ps[0:n, :], AF.Square, accum_out=ss[0:n, 0:1])
            P.act(ss[0:n, 1:2], ss[0:n, 0:1], AF.Sqrt, scale=1.0 / 1024, bias=self.epsc[0:n, 0:1])
            P.recip(ss[0:n, 1:2], ss[0:n, 1:2])
            yo = self.Rview((tc % 2) * 4096, F32, [1024])
            P.stt('dve', yo[0:n, :], ps[0:n, :], ss[0:n, 1:2], self.fgain_bc[0:n, :], ALU.mult, ALU.mult)
            P.dma('sp', dst, yo[0:n, :])


_CACHE = {}


def _get_nc(n_layers=4, n_pass=2):
    key = (n_layers, n_pass)
    if key not in _CACHE:
        _CACHE[key] = Builder(n_layers, n_pass).build()
    return _CACHE[key]


def make_in_maps(inp):
    f = lambda a: np.ascontiguousarray(np.asarray(a, dtype=np.float32))
    shared = {}
    shared['l0_norm'] = f(inp['l0_norm'])
    shared['l1_norm'] = f(inp['l1_norm'])
    shared['l2_norm'] = f(inp['l2_norm'])
    shared['l3_norm'] = f(inp['l3_norm'])
    shared['final_norm'] = f(inp['final_norm'])
    shared['l0_w_in'] = f(inp['l0_pool_w_in'])
    shared['l0_w_grp'] = f(inp['l0_pool_w_grp'])
    shared['l0_scale'] = f(inp['l0_pool_scale'])
    shared['l0_w_out'] = f(inp['l0_pool_w_out'])
    shared['l3_w_in'] = f(inp['l3_pool_w_in'])
    shared['l3_w_grp'] = f(inp['l3_pool_w_grp'])
    shared['l3_scale'] = f(inp['l3_pool_scale'])
    shared['l3_w_out'] = f(inp['l3_pool_w_out'])
    shared['l1_w_in'] = f(inp['l1_sgu_w_in'])
    shared['l1_ln_g'] = f(inp['l1_sgu_ln_g'])
    shared['l1_ln_b'] = f(inp['l1_sgu_ln_b'])
    shared['l1_w_s'] = f(inp['l1_sgu_w_s'])
    shared['l1_b_s'] = f(inp['l1_sgu_b_s'])
    shared['l1_w_out'] = f(inp['l1_sgu_w_out'])
    shared['l2_w_in'] = f(inp['l2_dn_w_in'])
    shared['l2_conv_w'] = f(inp['l2_dn_conv_w'])
    shared['l2_a_log'] = f(inp['l2_dn_a_log'])
    shared['l2_dt_bias'] = f(inp['l2_dn_dt_bias'])
    shared['l2_o_gain'] = f(inp['l2_dn_o_gain'])
    shared['l2_w_out'] = f(inp['l2_dn_w_out'])
    maps = []
    for c in range(8):
        m = dict(shared)
        m['xp'] = f(inp['x_prompt'][c])
        m['xs'] = f(inp['x_sample'][c * NS:(c + 1) * NS, 0, :])
        m['sp0'] = f(inp['state_pool_l0'][c * NS:(c + 1) * NS])
        m['sc2'] = f(inp['state_conv_l2'][c * NS:(c + 1) * NS])
        m['sd2'] = f(inp['state_delta_l2'][c * NS:(c + 1) * NS])
        m['sp3'] = f(inp['state_pool_l3'][c * NS:(c + 1) * NS])
        maps.append(m)
    return maps


def gather(res):
    r = res.results
    cat = lambda k: np.concatenate([np.asarray(r[c][k]) for c in range(8)], axis=0)
    stk = lambda k: np.stack([np.asarray(r[c][k]) for c in range(8)], axis=0)
    return (
        stk('yp').astype(np.float32),
        cat('ys').reshape(128, 1, 1024).astype(np.float32),
        stk('p0p').astype(np.float32),
        cat('p0s').astype(np.float32),
        cat('v1s').reshape(128, 1, 2048).astype(np.float32),
        stk('c2p').astype(np.float32),
        cat('c2s').astype(np.float32),
        stk('d2p').astype(np.float32),
        cat('d2s').astype(np.float32),
        stk('p3p').astype(np.float32),
        cat('p3s').astype(np.float32),
    )


def kernel(**inputs):
    nc = _get_nc()
    res = run_bass_kernel_spmd(nc, make_in_maps(inputs), core_ids=list(range(8)))
    return gather(res)
```

```python
import numpy as np
import concourse.bass as bass
import concourse.mybir as mybir
from concourse.bass_utils import run_bass_kernel_spmd

F32 = mybir.dt.float32
BF16 = mybir.dt.bfloat16
U8 = mybir.dt.uint8
AF = mybir.ActivationFunctionType
ALU = mybir.AluOpType
EPS = 1e-6
ENGS = ['pe', 'act', 'dve', 'pool', 'sp']
EI = {e: i for i, e in enumerate(ENGS)}
DMA_ROW = 5
BLK = 32
NDSEM = 8
ESZ = {F32: 4, BF16: 2, U8: 1}

NCOL = 1040
SEQ = 2048
PASS = 1024
NS = 16


def _esize(dt):
    return ESZ[dt]


class Prog:
    def __init__(self, nc, arenas):
        self.nc = nc
        self.arenas = arenas
        self.ops = {e: [] for e in ENGS}
        self.Wr = {a: np.zeros((6, (b + BLK - 1) // BLK + 1), np.int64) for a, b in arenas.items()}
        self.Rd = {a: np.zeros((6, (b + BLK - 1) // BLK + 1), np.int64) for a, b in arenas.items()}
        self.ndma = 0
        self.dma_info = []
        self.dma_byq = {'sp': [], 'pool': []}
        self.tokcache = {}

    def toks(self, ap):
        name = ap.tensor.name
        if name not in self.arenas:
            return None
        key = (name, ap.offset, ap.ap, str(ap.dtype))
        r = self.tokcache.get(key)
        if r is not None:
            return r
        es = _esize(ap.dtype)
        dims = list(ap.ap)
        pstride = dims[0][0]
        off = ap.offset % pstride if pstride else ap.offset
        free = dims[1:]
        ivs = []

        def rec(d, o):
            s, n = free[d]
            if d == len(free) - 1:
                if s == 0 or n == 1:
                    ivs.append((o, o + 1))
                else:
                    ivs.append((o, o + (n - 1) * s + 1))
                return
            for i in range(n if s != 0 else 1):
                rec(d + 1, o + i * s)
        if free:
            rec(0, off)
        else:
            ivs.append((off, off + 1))
        blks = sorted(set((lo * es // BLK, (hi * es + BLK - 1) // BLK) for lo, hi in ivs))
        merged = []
        for lo, hi in blks:
            if merged and lo <= merged[-1][1]:
                merged[-1][1] = max(merged[-1][1], hi)
            else:
                merged.append([lo, hi])
        r = (name, [(a, b) for a, b in merged])
        self.tokcache[key] = r
        return r

    def add(self, eng, fn, reads, writes, dma=False):
        ei = EI[eng]
        idx = len(self.ops[eng]) + 1
        raw = np.zeros(6, np.int64)
        oth = np.zeros(6, np.int64)
        dmadeps = set()
        rt = [t for t in (self.toks(a) for a in reads if a is not None) if t]
        wt = [t for t in (self.toks(a) for a in writes if a is not None) if t]
        BB = 2048 // BLK
        pw = []
        for name, ivs in rt + wt:
            if name == 'psum':
                pw.append((name, sorted(set((lo // BB * BB, (hi + BB - 1) // BB * BB) for lo, hi in ivs))))
        rt = [t for t in rt if t[0] != 'psum']
        wt = [t for t in wt if t[0] != 'psum'] + pw
        for name, ivs in rt:
            W = self.Wr[name]
            for lo, hi in ivs:
                seg = W[:, lo:hi]
                raw = np.maximum(raw, seg.max(axis=1))
                if seg[DMA_ROW].any():
                    dmadeps.update(np.unique(seg[DMA_ROW]).tolist())
        for name, ivs in wt:
            W = self.Wr[name]
            R = self.Rd[name]
            for lo, hi in ivs:
                seg = W[:, lo:hi]
                seg2 = R[:, lo:hi]
                raw = np.maximum(raw, seg.max(axis=1))
                oth = np.maximum(oth, seg2.max(axis=1))
                if seg[DMA_ROW].any():
                    dmadeps.update(np.unique(seg[DMA_ROW]).tolist())
                if seg2[DMA_ROW].any():
                    dmadeps.update(np.unique(seg2[DMA_ROW]).tolist())
        dmadeps.discard(0)
        waits = {}
        for e in ENGS:
            j = EI[e]
            if e == eng and not dma:
                if eng == 'pe':
                    continue
                v = int(max(raw[j], oth[j]))
                if v and idx - v <= 16:
                    waits[e] = v
                continue
            v = int(max(raw[j], oth[j]))
            if v:
                waits[e] = v
        dn = None
        if dma:
            dn = self.ndma
            self.ndma += 1
            lst = self.dma_byq[eng]
            self.dma_info.append((eng, len(lst)))
            if len(lst) >= NDSEM:
                dmadeps.add(lst[len(lst) - NDSEM] + 1)
            lst.append(dn)
            row, val = DMA_ROW, dn + 1
        else:
            row, val = ei, idx
        for name, ivs in rt:
            R = self.Rd[name]
            for lo, hi in ivs:
                R[row, lo:hi] = val
        for name, ivs in wt:
            W = self.Wr[name]
            for lo, hi in ivs:
                W[row, lo:hi] = val
        import sys as _sys
        fr = _sys._getframe(2)
        if fr.f_code.co_name in ('mm', 'tt', 'ts', 'stt', 'copy', 'act', 'memset', 'dma', 'transpose', 'recip', 'raw'):
            fr = fr.f_back
        self.ops[eng].append(dict(fn=fn, waits=waits, dwaits=sorted(d - 1 for d in dmadeps), dma=dn,
                                  line=fr.f_lineno))

    def mm(self, out, lhsT, rhs, start=True, stop=True):
        self.add('pe', lambda e: e.matmul(out, lhsT, rhs, start=start, stop=stop), [lhsT, rhs], [out])

    def transpose(self, out, in_, ident):
        self.add('pe', lambda e: e.transpose(out, in_, ident), [in_, ident], [out])

    def act(self, out, in_, func, scale=None, bias=None, accum_out=None, eng='act'):
        kw = {}
        rd = [in_]
        if scale is not None:
            kw['scale'] = scale
            if not isinstance(scale, (int, float)):
                rd.append(scale)
        if bias is not None:
            kw['bias'] = bias
            if not isinstance(bias, (int, float)):
                rd.append(bias)
        wr = [out]
        if accum_out is not None:
            kw['accum_out'] = accum_out
            wr.append(accum_out)
        self.add('act', lambda e: e.activation(out, in_, func, **kw), rd, wr)

    def tt(self, eng, out, in0, in1, op):
        self.add(eng, lambda e: e.tensor_tensor(out, in0, in1, op), [in0, in1], [out])

    def ts(self, eng, out, in0, s1, s2, op0, op1=None, accum_out=None):
        rd = [in0] + [s for s in (s1, s2) if s is not None and not isinstance(s, (int, float))]
        wr = [out] + ([accum_out] if accum_out is not None else [])
        kw = {}
        if op1 is not None:
            kw['op1'] = op1
        if accum_out is not None:
            kw['accum_out'] = accum_out
        self.add(eng, lambda e: e.tensor_scalar(out, in0, s1, s2, op0, **kw), rd, wr)

    def stt(self, eng, out, in0, scalar, in1, op0, op1):
        rd = [in0, in1] + ([scalar] if not isinstance(scalar, (int, float)) else [])
        eng = 'dve'
        self.add(eng, lambda e: e.scalar_tensor_tensor(out, in0, scalar, in1, op0, op1), rd, [out])

    def copy(self, eng, out, in_):
        if eng == 'act':
            self.add('act', lambda e: e.copy(out, in_), [in_], [out])
        else:
            self.add(eng, lambda e: e.tensor_copy(out, in_), [in_], [out])

    def memset(self, eng, ap, val):
        self.add(eng, lambda e: e.memset(ap, val), [], [ap])

    def dma(self, eng, out, in_):
        self.add(eng, lambda e: e.dma_start(out=out, in_=in_), [in_], [out], dma=True)

    def rsqrt(self, out, in_, scale, eps_ap):
        self.act(out, in_, AF.Ln, scale=scale, bias=eps_ap)
        self.act(out, out, AF.Exp, scale=-0.5)

    def recip(self, out, in_):
        self.add('dve', lambda e: e.reciprocal(out, in_), [in_], [out])

    def raw(self, eng, fn, reads, writes):
        self.add(eng, fn, reads, writes)

    def emit(self, stack):
        nc = self.nc
        need = {e: set() for e in ENGS}
        for e in ENGS:
            for op in self.ops[e]:
                for E, j in op['waits'].items():
                    need[E].add(j)
        cum = {}
        for e in ENGS:
            c = np.zeros(len(self.ops[e]) + 1, np.int64)
            k = 0
            for i in range(1, len(self.ops[e]) + 1):
                if i in need[e]:
                    k += 1
                c[i] = k
            cum[e] = c
        print("ops:", {e: len(self.ops[e]) for e in ENGS}, "signals:", {e: int(cum[e][-1]) for e in ENGS}, "ndma:", self.ndma)
        sems = {e: stack.enter_context(nc.semaphore("s_" + e)) for e in ENGS}
        dsems = {q: [stack.enter_context(nc.semaphore("d%s%d" % (q, i))) for i in range(NDSEM)]
                 for q in ('sp', 'pool')}
        dinfo = self.dma_info
        dbyq = self.dma_byq
        block = stack.enter_context(nc.Block())
        ndma = self.ndma
        prog = self

        def run(eng_name, e):
            waited = {E: 0 for E in ENGS}
            dwaited = {q: [0] * NDSEM for q in ('sp', 'pool')}
            ops = prog.ops[eng_name]
            for i, op in enumerate(ops, start=1):
                for E, j in op['waits'].items():
                    v = int(cum[E][j])
                    if v > waited[E]:
                        e.wait_ge(sems[E], v)
                        waited[E] = v
                for dn in op['dwaits']:
                    q, k = dinfo[dn]
                    s = k % NDSEM
                    v = 16 * (k // NDSEM + 1)
                    if v > dwaited[q][s]:
                        e.wait_ge(dsems[q][s], v)
                        dwaited[q][s] = v
                ins = op['fn'](e)
                if op['dma'] is not None:
                    q, k = dinfo[op['dma']]
                    ins.then_inc(dsems[q][k % NDSEM], 16)
                elif i in need[eng_name]:
                    ins.then_inc(sems[eng_name], 1)
            if eng_name == 'sp':
                for q in ('sp', 'pool'):
                    nq = len(dbyq[q])
                    for s in range(NDSEM):
                        cnt = len(range(s, nq, NDSEM))
                        if cnt and 16 * cnt > dwaited[q][s]:
                            e.wait_ge(dsems[q][s], 16 * cnt)

        @block.tensor
        def _(e):
            run('pe', e)

        @block.scalar
        def _(e):
            run('act', e)

        @block.vector
        def _(e):
            run('dve', e)

        @block.gpsimd
        def _(e):
            run('pool', e)

        @block.sync
        def _(e):
            run('sp', e)


class SB:
    def __init__(self, arena_ap, total):
        self.a = arena_ap
        self.total = total
        self.off = 0

    def alloc(self, nbytes):
        o = self.off
        self.off = (o + nbytes + 63) // 64 * 64
        assert self.off <= self.total, ("SBUF overflow", self.off, self.total)
        return o

    def view(self, off, dt, shape, parts=128, p0=0):
        n = int(np.prod(shape))
        v = self.a[p0:p0 + parts, off:off + n * _esize(dt)].bitcast(dt)
        if len(shape) == 2:
            v = v.rearrange("p (a b) -> p a b", a=shape[0])
        elif len(shape) == 3:
            v = v.rearrange("p (a b c) -> p a b c", a=shape[0], b=shape[1])
        return v

    def new(self, dt, shape, parts=128):
        off = self.alloc(int(np.prod(shape)) * _esize(dt))
        return self.view(off, dt, shape, parts)


ARENA_BYTES = 212000


class Builder:
    def __init__(self, n_layers=4, n_pass=2):
        self.n_layers = n_layers
        self.n_pass = n_pass

    def dram_in(self, name, shape):
        return self.nc.dram_tensor(name, list(shape), F32, kind="ExternalInput").ap()

    def dram_out(self, name, shape):
        return self.nc.dram_tensor(name, list(shape), F32, kind="ExternalOutput").ap()

    def build(self):
        from contextlib import ExitStack
        nc = bass.Bass("TRN2", target_bir_lowering=False)
        self.nc = nc
        D = {}
        D['xp'] = self.dram_in('xp', (SEQ, 1024))
        D['xs'] = self.dram_in('xs', (NS, 1024))
        D['sp0'] = self.dram_in('sp0', (NS, 15, 2048))
        D['sc2'] = self.dram_in('sc2', (NS, 3, 3072))
        D['sd2'] = self.dram_in('sd2', (NS, 8, 128, 128))
        D['sp3'] = self.dram_in('sp3', (NS, 15, 2048))
        for i in range(4):
            D['l%d_norm' % i] = self.dram_in('l%d_norm' % i, (1024,))
        D['final_norm'] = self.dram_in('final_norm', (1024,))
        for i in (0, 3):
            D['l%d_w_in' % i] = self.dram_in('l%d_w_in' % i, (1024, 4096))
            D['l%d_w_grp' % i] = self.dram_in('l%d_w_grp' % i, (4, 512, 512))
            D['l%d_scale' % i] = self.dram_in('l%d_scale' % i, (2048,))
            D['l%d_w_out' % i] = self.dram_in('l%d_w_out' % i, (2048, 1024))
        D['l1_w_in'] = self.dram_in('l1_w_in', (1024, 6144))
        D['l1_ln_g'] = self.dram_in('l1_ln_g', (2048,))
        D['l1_ln_b'] = self.dram_in('l1_ln_b', (2048,))
        D['l1_w_s'] = self.dram_in('l1_w_s', (4, 128, 128))
        D['l1_b_s'] = self.dram_in('l1_b_s', (4, 128))
        D['l1_w_out'] = self.dram_in('l1_w_out', (2048, 1024))
        D['l2_w_in'] = self.dram_in('l2_w_in', (1024, 4112))
        D['l2_conv_w'] = self.dram_in('l2_conv_w', (4, 3072))
        D['l2_a_log'] = self.dram_in('l2_a_log', (8,))
        D['l2_dt_bias'] = self.dram_in('l2_dt_bias', (8,))
        D['l2_o_gain'] = self.dram_in('l2_o_gain', (128,))
        D['l2_w_out'] = self.dram_in('l2_w_out', (1024, 1024))
        O = {}
        O['yp'] = self.dram_out('yp', (SEQ, 1024))
        O['ys'] = self.dram_out('ys', (NS, 1024))
        O['p0p'] = self.dram_out('p0p', (15, 2048))
        O['p0s'] = self.dram_out('p0s', (NS, 15, 2048))
        O['v1s'] = self.dram_out('v1s', (NS, 2048))
        O['c2p'] = self.dram_out('c2p', (3, 3072))
        O['c2s'] = self.dram_out('c2s', (NS, 3, 3072))
        O['d2p'] = self.dram_out('d2p', (8, 128, 128))
        O['d2s'] = self.dram_out('d2s', (NS, 8, 128, 128))
        O['p3p'] = self.dram_out('p3p', (15, 2048))
        O['p3s'] = self.dram_out('p3s', (NS, 15, 2048))
        self.D, self.O = D, O

        with ExitStack() as stack:
            arena = stack.enter_context(nc.sbuf_tensor("sb", [128, ARENA_BYTES], U8))
            psum = stack.enter_context(nc.psum_tensor("psum", [128, 4096], F32))
            self.P = Prog(nc, {"sb": ARENA_BYTES, "psum": 16384})
            self.sb = SB(arena, ARENA_BYTES)
            self.psum = psum
            self.pbank = 0
            self.program()
            self.P.emit(stack)
        return nc

    def bank(self, n=1):
        nb = getattr(self, 'bank_n', 8)
        b = self.pbank
        if b % n:
            b += n - b % n
        if b + n > nb:
            b = 0
        self.pbank = (b + n) % nb
        return self.psum[:, b * 512:(b + n) * 512]

    def program(self):
        P, sb, D, O = self.P, self.sb, self.D, self.O
        self.xT = sb.new(F32, [8, NCOL])
        self.hT = sb.new(BF16, [8, NCOL])
        self.B1 = sb.new(BF16, [16, NCOL])
        self.Roff = sb.alloc(61440)
        self.Woff = [sb.alloc(16384) for _ in range(3)]
        self.wslot_i = 0
        self.ident = sb.new(F32, [128])
        self.identb = sb.new(BF16, [128])
        self.onesb = sb.new(BF16, [128])
        self.halo = {0: sb.new(F32, [16, 15]), 3: sb.new(F32, [16, 15])}
        self.gain = {i: sb.new(F32, [8]) for i in range(4)}
        self.pscale = {0: sb.new(F32, [16]), 3: sb.new(F32, [16])}
        self.stage_small = sb.new(F32, [128])
        self.epsc = sb.new(F32, [1])
        self.rs_n = sb.new(F32, [512])
        print("SBUF persistent bytes:", sb.off)
        P.memset('pool', self.epsc, EPS)

        P.memset('pool', self.ident, 1.0)
        P.raw('pool', lambda e: e.affine_select(out=self.ident, in_=self.ident, pattern=[[-1, 128]],
                                               compare_op=ALU.is_equal, fill=0.0, base=0, channel_multiplier=1),
              [self.ident], [self.ident])
        P.copy('pool', self.identb, self.ident)
        P.memset('pool', self.onesb, 1.0)
        for i in range(4):
            self.load_pm(D['l%d_norm' % i], 1024, self.gain[i])
        for i in (0, 3):
            self.load_pm(D['l%d_scale' % i], 2048, self.pscale[i])

        if self.n_layers >= 2:
            self.sgu_setup()
        if self.n_layers >= 3:
            self.dn_setup()
        for p in range(self.n_pass):
            self.p = p
            self.tiles = [(0, 512), (512, 512)] + ([(1024, NS)] if p == 0 else [])
            self.load_x(p)
            for li in range(self.n_layers):
                self.norm_li = li
                self.norm_done = set()
                self.ensure_h(0)
                if li in (0, 3):
                    self.pool_layer(li)
                elif li == 1:
                    self.sgu_layer()
                else:
                    self.dn_layer()
            self.final_out(p)

    def Rview(self, off, dt, shape, parts=128, p0=0):
        return self.sb.view(self.Roff + off, dt, shape, parts, p0)

    def load_pm(self, dram1d, n, dest):
        P = self.P
        r = n // 128
        st = self.stage_small[0:r, :]
        P.dma('sp', st, dram1d.rearrange("(c p) -> c p", p=128))
        ps = self.bank()
        P.transpose(ps[:, 0:r], st, self.ident[0:r, 0:r])
        P.copy('dve', dest, ps[:, 0:r])

    def wslot(self):
        i = self.wslot_i
        self.wslot_i = (i + 1) % 3
        return self.Woff[i]

    def load_x(self, p):
        P = self.P
        st_off = 0
        for tc in range(8):
            st = self.Rview(32768 + (tc % 2) * 4096, F32, [1024])
            P.dma('sp', st, self.D['xp'][p * PASS + tc * 128: p * PASS + (tc + 1) * 128, :])
            ps = self.bank(2)
            for k in range(8):
                P.transpose(ps[:, k * 128:(k + 1) * 128], st[:, k * 128:(k + 1) * 128], self.ident)
            P.copy('act' if tc % 2 else 'dve', self.xT[:, :, tc * 128:(tc + 1) * 128],
                   ps.rearrange("p (k t) -> p k t", k=8))
        if p == 0:
            st = self.Rview(8192, F32, [1024], parts=NS)
            P.dma('sp', st, self.D['xs'])
            ps = self.bank(1)
            for k in range(8):
                P.transpose(ps[:, k * NS:(k + 1) * NS], st[:, k * 128:(k + 1) * 128], self.ident[0:NS, 0:NS])
            P.copy('dve', self.xT[:, :, 1024:1024 + NS], ps[:, 0:8 * NS].rearrange("p (k t) -> p k t", k=8))

    def ensure_h(self, ti):
        if ti in self.norm_done:
            return
        self.norm_done.add(ti)
        P = self.P
        li = self.norm_li
        c0, n = self.tiles[ti]
        sq = self.B1[:, 0:8, c0:c0 + n]
        rs = self.rs_n
        P.act(sq, self.xT[:, :, c0:c0 + n], AF.Square)
        ps = self.bank()
        for k in range(8):
            P.mm(ps[:, 0:n], self.onesb, sq[:, k, :], start=(k == 0), stop=(k == 7))
        P.rsqrt(rs[:, 0:n], ps[:, 0:n], 1.0 / 1024, self.epsc[:, 0:1])
        for k in range(8):
            P.stt('dve', self.hT[:, k, c0:c0 + n], self.xT[:, k, c0:c0 + n],
                  self.gain[li][:, k:k + 1], rs[:, 0:n], ALU.mult, ALU.mult)

    def pool_layer(self, li):
        P, D, O = self.P, self.D, self.O
        w_in = D['l%d_w_in' % li].rearrange("(k p) n -> p k n", p=128)
        w_out = D['l%d_w_out' % li].rearrange("(k p) n -> p k n", p=128)
        scale = self.pscale[li]
        halo = self.halo[li]
        st_in = D['sp0' if li == 0 else 'sp3']
        st_out_p = O['p0p' if li == 0 else 'p3p']
        st_out_s = O['p0s' if li == 0 else 'p3s']
        XE = 4 * 527 * 4
        xoff, taoff, tboff = 0, 8448, 16896
        szoff = 25344
        pooff = 33536
        hsoff = 37632
        pending = None
        for g in range(4):
            w = 2 << g
            wo = self.wslot()
            ws = self.sb.view(wo, BF16, [8, 1024])
            P.dma('pool', ws[:, :, 0:512], w_in[:, :, g * 512:(g + 1) * 512])
            P.dma('pool', ws[:, :, 512:1024], w_in[:, :, 2048 + g * 512:2048 + (g + 1) * 512])
            go = self.wslot()
            wg = self.sb.view(go, BF16, [4, 512])
            P.dma('pool', wg, D['l%d_w_grp' % li][g].rearrange("(k p) n -> p k n", p=128))
            for ti, (c0, n) in enumerate(self.tiles):
                self.ensure_h(ti)
                samp = (n == NS)
                if samp:
                    nseq, T = NS, 1
                else:
                    nseq, T = 1, n
                L = 15 + T
                alt = (g * len(self.tiles) + ti) % 2

                def V4(off, dt=F32):
                    return self.Rview(off, dt, [4, nseq, L])
                xe, ta, tb = V4(39680 if alt else xoff), V4(taoff), V4(tboff)
                sz = self.Rview(48128 if alt else szoff, F32, [4, 512])
                po = self.Rview(56320 if alt else pooff, BF16, [4, 512])
                if samp:
                    for half in range(2):
                        hs = self.Rview(hsoff, F32, [512], parts=120)
                        P.dma('sp', hs, st_in[half * 8:(half + 1) * 8, :, g * 512:(g + 1) * 512]
                              .rearrange("b r c -> (b r) c"))
                        ps = self.bank()
                        for j in range(4):
                            P.transpose(ps[:, j * 120:(j + 1) * 120], hs[:, j * 128:(j + 1) * 128],
                                        self.ident[0:120, 0:120])
                        P.copy('dve', xe[:, :, half * 8:(half + 1) * 8, 0:15],
                               ps[:, 0:480].rearrange("p (j b r) -> p j b r", j=4, b=8))
                else:
                    if self.p == 0 and ti == 0:
                        P.memset('pool', xe[:, :, :, 0:15], 0.0)
                    else:
                        P.copy('pool', xe[:, :, 0, 0:15], halo[:, g * 4:(g + 1) * 4, :])
                for j in range(4):
                    ps = self.bank()
                    for k in range(8):
                        P.mm(ps[:, 0:n], ws[:, k, j * 128:(j + 1) * 128], self.hT[:, k, c0:c0 + n],
                             start=(k == 0), stop=(k == 7))
                    if samp:
                        P.copy('act', xe[:, j, :, 15], ps[:, 0:n])
                    else:
                        P.copy('act', xe[:, j, 0, 15:15 + n], ps[:, 0:n])
                for j in range(4):
                    ps = self.bank()
                    for k in range(8):
                        P.mm(ps[:, 0:n], ws[:, k, 512 + j * 128:512 + (j + 1) * 128], self.hT[:, k, c0:c0 + n],
                             start=(k == 0), stop=(k == 7))
                    P.act(sz[:, j, 0:n], ps[:, 0:n], AF.Silu)
                if not samp:
                    P.copy('pool', halo[:, g * 4:(g + 1) * 4, :], xe[:, :, 0, T:T + 15])
                src = xe
                bufs = [ta, tb]
                sh = 1
                lvl = 0
                while sh < w:
                    dst = bufs[lvl % 2]
                    lo = 2 * sh - 1
                    P.tt('dve', dst[:, :, :, lo:L], src[:, :, :, lo:L],
                         src[:, :, :, lo - sh:L - sh], ALU.add)
                    src = dst
                    sh *= 2
                    lvl += 1
                if samp:
                    pov = po[:, :, 0:n]
                    P.stt('dve', pov, src[:, :, :, 15], 1.0 / w, xe[:, :, :, 15], ALU.mult, ALU.subtract)
                else:
                    P.stt('dve', po[:, :, 0:n], src[:, :, 0, 15:15 + n], 1.0 / w, xe[:, :, 0, 15:15 + n],
                          ALU.mult, ALU.subtract)
                    if self.p == 0 and ti == 0:
                        for t in range(w - 1):
                            P.stt('pool', po[:, :, t:t + 1], src[:, :, 0, 15 + t:16 + t], 1.0 / (t + 1),
                                  xe[:, :, 0, 15 + t:16 + t], ALU.mult, ALU.subtract)
                def grouped(g=g, c0=c0, n=n, samp=samp, wg=wg, po=po, sz=sz, xe=xe):
                    for j in range(4):
                        ps = self.bank()
                        for k in range(4):
                            P.mm(ps[:, 0:n], wg[:, k, j * 128:(j + 1) * 128], po[:, k, 0:n],
                                 start=(k == 0), stop=(k == 3))
                        c = g * 4 + j
                        P.stt('dve', self.B1[:, c, c0:c0 + n], ps[:, 0:n], scale[:, c:c + 1], sz[:, j, 0:n],
                              ALU.mult, ALU.mult)
                    if samp:
                        ps = self.bank()
                        for j in range(4):
                            P.transpose(ps[0:NS, j * 128:(j + 1) * 128], xe[:, j, :, 15], self.ident)
                        so = self.Rview(hsoff, F32, [512], parts=NS)
                        P.copy('dve', so, ps[0:NS, 0:512])
                        P.dma('sp', st_out_s[:, 14, g * 512:(g + 1) * 512], so)
                if pending is not None:
                    pending()
                pending = grouped
        pending()
        if self.p == 0:
            P.dma('sp', st_out_s[:, 0:14, :], st_in[:, 1:15, :])
        if self.p == self.n_pass - 1:
            ps = self.bank(4)
            for c in range(16):
                P.transpose(ps[0:15, c * 128:(c + 1) * 128], halo[:, c, :], self.ident)
            so = self.Rview(0, F32, [2048], parts=15)
            P.copy('dve', so, ps[0:15, :])
            P.dma('sp', st_out_p, so)
        self.out_proj(w_out, 16, self.B1)

    def out_proj(self, w_out, nk, src):
        P = self.P
        for q in range(4):
            oo = self.wslot()
            wo_ = self.sb.view(oo, BF16, [nk, 256])
            P.dma('pool', wo_, w_out[:, :, q * 256:(q + 1) * 256])
            for (c0, n) in self.tiles:
                for dd in range(2):
                    d = q * 2 + dd
                    ps = self.bank()
                    for k in range(nk):
                        P.mm(ps[:, 0:n], wo_[:, k, dd * 128:(dd + 1) * 128], src[:, k, c0:c0 + n],
                             start=(k == 0), stop=(k == nk - 1))
                    P.tt('dve', self.xT[:, d, c0:c0 + n], self.xT[:, d, c0:c0 + n], ps[:, 0:n], ALU.add)

    def sgu_setup(self):
        P, D, sb = self.P, self.D, self.sb
        self.wsT = sb.new(BF16, [4, 128])
        self.wsTs = sb.new(BF16, [4, NS], parts=NS)
        self.bsr = sb.new(F32, [4, 128], parts=1)
        self.bss = sb.new(F32, [4, NS], parts=1)
        self.bs2 = sb.new(BF16, [512], parts=2)
        self.bss2 = sb.new(BF16, [4, NS], parts=2)
        self.onesf = sb.new(F32, [128])
        self.w00 = sb.new(F32, [4], parts=NS)
        P.memset('pool', self.onesf, 1.0)
        wsn = self.Rview(0, F32, [4, 128])
        P.dma('sp', wsn, D['l1_w_s'].rearrange("g i j -> i g j"))
        for g in range(4):
            P.raw('pool', (lambda g: lambda e: e.affine_select(out=wsn[:, g, :], in_=wsn[:, g, :], pattern=[[-1, 128]],
                                                          compare_op=ALU.is_ge, fill=0.0, base=0,
                                                          channel_multiplier=1))(g),
                  [wsn[:, g, :]], [wsn[:, g, :]])
        ps = self.bank()
        for g in range(4):
            P.transpose(ps[:, g * 128:(g + 1) * 128], wsn[:, g, :], self.ident)
        P.copy('dve', self.wsT, ps.rearrange("p (g i) -> p g i", g=4))
        P.dma('sp', self.bsr, D['l1_b_s'].rearrange("(o g) i -> o g i", o=1))
        P.copy('dve', self.bss, self.bsr[:, :, 0:1].to_broadcast([1, 4, NS]))
        bs2f = self.Rview(4096, F32, [512], parts=2)
        lo_f = self.Rview(8192, F32, [512], parts=2)
        hi_b = self.Rview(12288, BF16, [512], parts=2)
        lo_b = self.Rview(14336, BF16, [512], parts=2)
        P.dma('sp', bs2f, D['l1_b_s'].rearrange("(o g) i -> o (g i)", o=1).to_broadcast([2, 512]))
        P.copy('dve', hi_b, bs2f)
        P.tt('dve', lo_f, bs2f, hi_b, ALU.subtract)
        P.copy('dve', lo_b, lo_f)
        P.copy('dve', self.bs2, hi_b)
        P.dma('sp', self.bs2[1:2, :], lo_b[1:2, :])
        P.copy('dve', self.bss2, self.bs2.rearrange("p (g i) -> p g i", g=4)[:, :, 0:1].to_broadcast([2, 4, NS]))
        for g in range(4):
            P.dma('sp', self.w00[:, g:g + 1], D['l1_w_s'][g, 0:1, 0:1].to_broadcast([NS, 1]))
        for g in range(4):
            P.ts('dve', self.wsTs[:, g, :], self.ident[0:NS, 0:NS], self.w00[:, g:g + 1], None, ALU.mult)

    def sgu_layer(self):
        P, D, O = self.P, self.D, self.O
        w_in = D['l1_w_in'].rearrange("(k p) n -> p k n", p=128)
        w_out = D['l1_w_out'].rearrange("(k p) n -> p k n", p=128)
        gbc = self.Rview(0, F32, [2048])
        bbc = self.Rview(8192, F32, [2048])
        vraw = self.Rview(16384, F32, [2048])
        vn = self.Rview(24576, BF16, [2048])
        uz = self.Rview(28672, F32, [4, 512])
        uu = self.Rview(36864, F32, [4, 512])
        st = self.Rview(45056, F32, [8])
        P.dma('sp', gbc, D['l1_ln_g'].rearrange("(o n) -> o n", o=1).to_broadcast([128, 2048]))
        P.dma('sp', bbc, D['l1_ln_b'].rearrange("(o n) -> o n", o=1).to_broadcast([128, 2048]))
        wv = []
        for hh in range(2):
            o = self.wslot()
            v = self.sb.view(o, BF16, [8, 1024])
            P.dma('pool', v[:, :, 0:512], w_in[:, :, 2048 + hh * 1024:2048 + hh * 1024 + 512])
            P.dma('pool', v[:, :, 512:1024], w_in[:, :, 2048 + hh * 1024 + 512:2048 + (hh + 1) * 1024])
            wv.append(v)
        chunks = [(tc * 128, 128) for tc in range(8)] + ([(1024, NS)] if self.p == 0 else [])
        vr2 = self.Rview(45120, F32, [2048])
        vn2 = self.Rview(53312, BF16, [2048])
        st2 = self.Rview(57408, F32, [8])
        vbufs = [(vraw, vn, st), (vr2, vn2, st2)]
        def vmm(ci):
            c0, n = chunks[ci]
            self.ensure_h(min(c0 // 512, 2))
            ps4 = self.psum[:, (ci % 2) * 2048:(ci % 2 + 1) * 2048]
            for s_ in range(4):
                for k in range(8):
                    P.mm(ps4[0:n, s_ * 512:(s_ + 1) * 512], self.hT[:, k, c0:c0 + n],
                         wv[s_ // 2][:, k, (s_ % 2) * 512:(s_ % 2 + 1) * 512], start=(k == 0), stop=(k == 7))

        def rest(ci):
            c0, n = chunks[ci]
            vraw, vn, st = vbufs[ci % 2]
            samp = (n == NS)
            ps4 = self.psum[:, (ci % 2) * 2048:(ci % 2 + 1) * 2048]
            P.act(vraw[0:n, :], ps4[0:n, :], AF.Gelu, accum_out=st[0:n, 0:1])
            P.ts('dve', st[0:n, 2:3], st[0:n, 0:1], -1.0 / 2048, None, ALU.mult)
            P.ts('dve', vraw[0:n, :], vraw[0:n, :], st[0:n, 2:3], None, ALU.add)
            P.act(vn[0:n, :], vraw[0:n, :], AF.Square, accum_out=st[0:n, 1:2])
            P.act(st[0:n, 3:4], st[0:n, 1:2], AF.Sqrt, scale=1.0 / 2048, bias=self.epsc[0:n, 0:1])
            P.recip(st[0:n, 3:4], st[0:n, 3:4])
            P.stt('dve', vraw[0:n, :], vraw[0:n, :], st[0:n, 3:4], gbc[0:n, :], ALU.mult, ALU.mult)
            if samp:
                P.tt('dve', vraw[0:n, :], vraw[0:n, :], bbc[0:n, :], ALU.add)
                P.copy('dve', vn[0:n, :], vraw[0:n, :])
                P.dma('sp', O['v1s'], vraw[0:n, :])
            else:
                P.tt('dve', vn[0:n, :], vraw[0:n, :], bbc[0:n, :], ALU.add)
            pss = ps4
            for cc in range(16):
                g = cc // 4
                o_ = pss[:, cc * n:(cc + 1) * n]
                if samp:
                    P.mm(o_, vn[0:n, cc * 128:(cc + 1) * 128], self.wsTs[:, g, :], start=True, stop=False)
                    P.mm(o_, self.onesb[0:2, :], self.bss2[:, g, :], start=False, stop=True)
                else:
                    P.mm(o_, vn[:, cc * 128:(cc + 1) * 128], self.wsT[:, g, :], start=True, stop=False)
                    P.mm(o_, self.onesb[0:2, :], self.bs2[:, g * 128:(g + 1) * 128], start=False, stop=True)
            if samp:
                P.copy('act', self.B1[:, :, c0:c0 + n], pss[:, 0:16 * n].rearrange("p (c i) -> p c i", c=16))
            else:
                for b in range(4):
                    P.copy('act' if b % 2 == 0 else 'dve', self.B1[:, 4 * b:4 * b + 4, c0:c0 + n],
                           pss[:, b * 512:(b + 1) * 512].rearrange("p (c i) -> p c i", c=4))

        vmm(0)
        for ci in range(len(chunks)):
            if ci + 1 < len(chunks):
                vmm(ci + 1)
            rest(ci)
        for g in range(4):
            o = self.wslot()
            wz = self.sb.view(o, BF16, [8, 1024])
            P.dma('pool', wz[:, :, 0:512], w_in[:, :, g * 512:(g + 1) * 512])
            P.dma('pool', wz[:, :, 512:1024], w_in[:, :, 4096 + g * 512:4096 + (g + 1) * 512])
            for ti, (c0, n) in enumerate(self.tiles):
                if (g * len(self.tiles) + ti) % 2:
                    uz = self.Rview(16384, F32, [4, 512])
                    uu = self.Rview(45120, F32, [4, 512])
                else:
                    uz = self.Rview(28672, F32, [4, 512])
                    uu = self.Rview(36864, F32, [4, 512])
                for j in range(4):
                    ps = self.bank()
                    for k in range(8):
                        P.mm(ps[:, 0:n], wz[:, k, j * 128:(j + 1) * 128], self.hT[:, k, c0:c0 + n],
                             start=(k == 0), stop=(k == 7))
                    P.act(uu[:, j, 0:n], ps[:, 0:n], AF.Gelu)
                for j in range(4):
                    ps = self.bank()
                    for k in range(8):
                        P.mm(ps[:, 0:n], wz[:, k, 512 + j * 128:512 + (j + 1) * 128], self.hT[:, k, c0:c0 + n],
                             start=(k == 0), stop=(k == 7))
                    P.act(uz[:, j, 0:n], ps[:, 0:n], AF.Silu)
                P.tt('dve', uz[:, :, 0:n], uz[:, :, 0:n], uu[:, :, 0:n], ALU.mult)
                P.tt('dve', self.B1[:, g * 4:(g + 1) * 4, c0:c0 + n], self.B1[:, g * 4:(g + 1) * 4, c0:c0 + n],
                     uz[:, :, 0:n], ALU.mult)
        self.out_proj(w_out, 16, self.B1)

    def dn_setup(self):
        P, D, sb = self.P, self.D, self.sb
        self.S = sb.new(F32, [8, 128])
        self.haloc = sb.new(F32, [24, 3])
        self.convw = sb.new(F32, [96])
        self.ogain = sb.new(F32, [1])
        self.dtb = sb.new(F32, [1], parts=8)
        self.nA = sb.new(F32, [1], parts=8)
        self.load_pm(D['l2_conv_w'].rearrange("j c -> (j c)"), 4 * 3072, self.convw)
        P.dma('sp', self.ogain, D['l2_o_gain'].rearrange("(p o) -> p o", o=1))
        P.dma('sp', self.dtb, D['l2_dt_bias'].rearrange("(p o) -> p o", o=1))
        P.dma('sp', self.nA, D['l2_a_log'].rearrange("(p o) -> p o", o=1))
        P.act(self.nA, self.nA, AF.Exp)
        P.ts('dve', self.nA, self.nA, -1.0, None, ALU.mult)
        P.memset('pool', self.S, 0.0)
        print("SBUF bytes after dn_setup:", sb.off)

    def dn_layer(self):
        P, D, O = self.P, self.D, self.O
        w_in = D['l2_w_in'].rearrange("(k p) n -> p k n", p=128)
        w_out = D['l2_w_out'].rearrange("(k p) n -> p k n", p=128)
        qT = self.B1[:, 0:8, :]
        kT = self.B1[:, 8:16, :]
        vT = self.Rview(0, BF16, [8, NCOL])
        sz = self.Rview(16640, BF16, [8, NCOL])
        WK = 33280
        G = self.Rview(WK + 19840, F32, [NCOL], parts=8)
        LB = self.Rview(WK + 19840 + 4160, F32, [NCOL], parts=8)
        accoff = WK + 8256
        sq = self.Rview(WK + 16448, BF16, [512])
        rs = self.Rview(WK + 17472, F32, [512])
        W0 = self.Woff[0] - self.Roff
        wl = [W0, W0 + 8192, W0 + 16384]
        xcoffs = [WK, W0 + 24576]
        accoffs = [WK + 8256, W0 + 32832]
        sq4 = self.Rview(W0 + 41024, BF16, [4, 512])
        unit = 0
        for s_ in range(8):
            ws = self.Rview(wl[s_ % 3], BF16, [8, 512])
            P.dma('pool', ws, w_in[:, :, s_ * 512:(s_ + 1) * 512])
            kind = 'qkvz'[s_ // 2]
            for ti, (c0, n) in enumerate(self.tiles):
                self.ensure_h(ti)
                samp = (n == NS)
                if samp:
                    nseq, T = NS, 1
                else:
                    nseq, T = 1, n
                L = 3 + T
                if kind != 'z':
                    unit += 1
                xcoff, accoff = xcoffs[unit % 2], accoffs[unit % 2]
                xc = self.Rview(xcoff, F32, [4, nseq, L])
                acc = self.Rview(accoff, F32, [4, 512])
                if kind != 'z':
                    if samp:
                        hs = self.Rview(accoff, F32, [512], parts=48)
                        P.dma('sp', hs, D['sc2'][:, :, s_ * 512:(s_ + 1) * 512].rearrange("b r c -> (b r) c"))
                        ps = self.bank()
                        for j in range(4):
                            P.transpose(ps[:, j * 48:(j + 1) * 48], hs[:, j * 128:(j + 1) * 128],
                                        self.ident[0:48, 0:48])
                        P.copy('dve', xc[:, :, :, 0:3], ps[:, 0:192].rearrange("p (j b r) -> p j b r", j=4, b=NS))
                    elif self.p == 0 and ti == 0:
                        P.memset('pool', xc[:, :, :, 0:3], 0.0)
                    else:
                        P.copy('pool', xc[:, :, 0, 0:3], self.haloc[:, s_ * 4:(s_ + 1) * 4, :])
                for j in range(4):
                    ps = self.bank()
                    for k in range(8):
                        P.mm(ps[:, 0:n], ws[:, k, j * 128:(j + 1) * 128], self.hT[:, k, c0:c0 + n],
                             start=(k == 0), stop=(k == 7))
                    if kind == 'z':
                        P.act(sz[:, (s_ - 6) * 4 + j, c0:c0 + n], ps[:, 0:n], AF.Silu)
                    elif samp:
                        P.copy('act', xc[:, j, :, 3], ps[:, 0:n])
                    else:
                        P.copy('act', xc[:, j, 0, 3:3 + n], ps[:, 0:n])
                if kind == 'z':
                    continue
                if not samp:
                    P.copy('pool', self.haloc[:, s_ * 4:(s_ + 1) * 4, :], xc[:, :, 0, T:T + 3])
                else:
                    ps = self.bank()
                    for j in range(4):
                        P.transpose(ps[0:NS, j * 128:(j + 1) * 128], xc[:, j, :, 3], self.ident)
                    so = self.Rview(WK + 17472, F32, [512], parts=NS)
                    P.copy('dve', so, ps[0:NS, 0:512])
                    P.dma('sp', O['c2s'][:, 2, s_ * 512:(s_ + 1) * 512], so)
                def xv(j, tap):
                    return xc[:, j, :, tap] if samp else xc[:, j, 0, tap:tap + n]
                pairs = ((0, 1), (2, 3))
                for pr in pairs:
                    for tap in range(4):
                        for j in pr:
                            ch = s_ * 4 + j
                            a_ = acc[:, j, 0:n]
                            if tap == 0:
                                P.ts('dve', a_, xv(j, 0), self.convw[:, ch:ch + 1], None, ALU.mult)
                            else:
                                P.stt('dve', a_, xv(j, tap), self.convw[:, tap * 24 + ch:tap * 24 + ch + 1], a_,
                                      ALU.mult, ALU.add)
                if kind == 'v':
                    for j in range(4):
                        P.act(vT[:, (s_ - 4) * 4 + j, c0:c0 + n], acc[:, j, 0:n], AF.Silu)
                    continue
                rs4 = self.Rview(xcoff, F32, [4, 512])
                for pr in pairs:
                    for j in pr:
                        P.act(acc[:, j, 0:n], acc[:, j, 0:n], AF.Silu)
                    for j in pr:
                        P.act(sq4[:, j, 0:n], acc[:, j, 0:n], AF.Square)
                    pss = {}
                    for j in pr:
                        ps = self.bank()
                        P.mm(ps[:, 0:n], self.onesb, sq4[:, j, 0:n])
                        pss[j] = ps
                    for j in pr:
                        P.act(rs4[:, j, 0:n], pss[j][:, 0:n], AF.Ln, bias=self.epsc[:, 0:1])
                    P.act(rs4[:, pr[0]:pr[1] + 1, 0:n], rs4[:, pr[0]:pr[1] + 1, 0:n], AF.Exp, scale=-0.5)
                    for j in pr:
                        hd = (s_ % 2) * 4 + j
                        if kind == 'q':
                            P.stt('dve', qT[:, hd, c0:c0 + n], acc[:, j, 0:n], 128.0 ** -0.5, rs4[:, j, 0:n],
                                  ALU.mult, ALU.mult)
                        else:
                            P.tt('dve', kT[:, hd, c0:c0 + n], acc[:, j, 0:n], rs4[:, j, 0:n], ALU.mult)
        o = self.wslot()
        wab = self.sb.view(o, BF16, [8, 16])
        P.dma('pool', wab, w_in[:, :, 4096:4112])
        for (c0, n) in self.tiles:
            ps = self.bank()
            for k in range(8):
                P.mm(ps[0:8, 0:n], wab[:, k, 0:8], self.hT[:, k, c0:c0 + n], start=(k == 0), stop=(k == 7))
            P.act(G[:, c0:c0 + n], ps[0:8, 0:n], AF.Exp, bias=self.dtb[:, 0:1])
            P.act(G[:, c0:c0 + n], G[:, c0:c0 + n], AF.Ln, bias=1.0)
            P.ts('dve', G[:, c0:c0 + n], G[:, c0:c0 + n], self.nA[:, 0:1], None, ALU.mult)
            ps = self.bank()
            for k in range(8):
                P.mm(ps[0:8, 0:n], wab[:, k, 8:16], self.hT[:, k, c0:c0 + n], start=(k == 0), stop=(k == 7))
            P.act(LB[:, c0:c0 + n], ps[0:8, 0:n], AF.Exp, scale=-1.0)
            P.act(LB[:, c0:c0 + n], LB[:, c0:c0 + n], AF.Ln, bias=1.0)
            P.ts('dve', LB[:, c0:c0 + n], LB[:, c0:c0 + n], -1.0, None, ALU.mult)
        if self.p == 0:
            P.dma('sp', O['c2s'][:, 0:2, :], D['sc2'][:, 1:3, :])
        if self.p == self.n_pass - 1:
            ps = self.bank(6)
            for c in range(24):
                P.transpose(ps[0:3, c * 128:(c + 1) * 128], self.haloc[:, c, :], self.ident)
            so = self.Rview(WK, F32, [3072], parts=3)
            P.copy('dve', so, ps[0:3, :])
            P.dma('sp', O['c2p'], so)

        dn_stop = 0
        om = self.hT
        W1 = self.Woff[1] - self.Roff

        def Wv(off, dt, shape, parts=128, p0=0):
            return self.Rview(W1 + off, dt, shape, parts, p0)
        Esel = self.Rview(WK + 0, F32, [8, 128], parts=8)
        Ucum = self.Rview(WK + 4096, F32, [128])
        Ublk = self.Rview(WK + 4608, F32, [128])
        Ua = self.Rview(WK + 5120, F32, [128])
        Ub = self.Rview(WK + 5632, F32, [128])
        maskS = self.Rview(WK + 6144, F32, [128])
        maskI = self.Rview(WK + 6656, F32, [128])
        colA = self.Rview(WK + 7168, F32, [16])
        ctmp = self.Rview(WK + 7232, F32, [40])
        EX = self.Rview(WK + 7392, F32, [40])
        GAMr = self.Rview(WK + 7552, F32, [128], parts=8)
        GLr = self.Rview(WK + 8064, F32, [128], parts=8)
        EGr = self.Rview(WK + 8576, F32, [128], parts=8)
        XA = self.Rview(WK + 9216, F32, [8, 128])
        Sb = self.Rview(WK + 13312, BF16, [8, 128])
        w_sb = self.Rview(WK + 15360, BF16, [8, 128])
        osq = self.Rview(WK + 17408, BF16, [8, 128])
        Abuf = [Wv(0, BF16, [4, 128]), Wv(1024, BF16, [4, 128])]
        Bbuf = [Wv(2048, BF16, [4, 128]), Wv(3072, BF16, [4, 128])]
        Tbuf = [Wv(4096, BF16, [4, 128]), Wv(5120, BF16, [4, 128])]
        TTf = Wv(6144, BF16, [8, 128])
        qk = Wv(8192, BF16, [8, 128])
        qkT = Wv(10240, BF16, [8, 128])
        kb = Wv(12288, BF16, [8, 128])
        kt = Wv(14336, BF16, [8, 128])
        bv = Wv(16384, BF16, [8, 128])
        nwkT = Wv(18432, BF16, [8, 128])
        qd = Wv(20480, BF16, [8, 128])
        tmpf = Wv(22528, F32, [8, 128])
        BIG = -30000.0
        P.memset('pool', Esel, 0.0)
        for h in range(8):
            P.copy('pool', Esel[:, h, :], self.ident[0:8, h:h + 1].to_broadcast([8, 128]))
        P.memset('pool', Ucum, 1.0)
        P.raw('pool', lambda e: e.affine_select(out=Ucum, in_=Ucum, pattern=[[1, 128]], compare_op=ALU.is_ge,
                                               fill=0.0, base=0, channel_multiplier=-1), [Ucum], [Ucum])
        P.memset('pool', Ucum[0:64, 64:128], 0.0)
        P.memset('pool', Ublk, 0.0)
        P.memset('pool', Ublk[0:64, 0:64], 1.0)
        P.memset('pool', Ublk[64:128, 64:128], 1.0)
        P.memset('pool', Ua, 0.0)
        P.memset('pool', Ua[0:64, :], 1.0)
        P.memset('pool', Ub, 0.0)
        P.memset('pool', Ub[64:128, :], 1.0)
        for m_, op_ in ((maskS, ALU.is_gt), (maskI, ALU.is_ge)):
            P.memset('pool', m_, 0.0)
            P.raw('pool', (lambda m_, op_: lambda e: e.affine_select(out=m_, in_=m_, pattern=[[-1, 128]],
                                                                   compare_op=op_, fill=BIG, base=0,
                                                                   channel_multiplier=1))(m_, op_), [m_], [m_])
            P.memset('pool', m_[64:128, 0:64], BIG)
        P.copy('act', Sb, self.S)
        self.bank_n = 6
        self.pbank = 0

        def finish(o_ps, c0, n):
            ov = o_ps.rearrange("p (h i) -> p h i", h=8)
            P.act(osq[:, :, 0:n], ov, AF.Square)
            ps = self.bank(2) if n == 128 else self.bank(1)
            for h in range(8):
                P.mm(ps[:, h * n:(h + 1) * n], self.onesb, osq[:, h, 0:n])
            psv = ps[:, 0:8 * n].rearrange("p (h i) -> p h i", h=8)
            P.rsqrt(XA[:, :, 0:n], psv, 1.0 / 128, self.epsc[:, 0:1])
            P.tt('dve', XA[:, :, 0:n], XA[:, :, 0:n], sz[:, :, c0:c0 + n], ALU.mult)
            P.stt('dve', om[:, :, c0:c0 + n], ov, self.ogain[:, 0:1], XA[:, :, 0:n], ALU.mult, ALU.mult)

        EXs = [EX, Wv(30720, F32, [40])]
        kts = [kt, self.Rview(W0 + 12288, BF16, [8, 128])]
        bvs = [bv, self.Rview(W0 + 10240, BF16, [8, 128])]
        gamc = self.Rview(WK + 9088, F32, [8])

        def front(d):
            c0 = d * 128
            cs = slice(c0, c0 + 128)
            EXd, ktd, bvd = EXs[d % 2], kts[d % 2], bvs[d % 2]
            psA = self.bank()
            P.transpose(psA[:, 0:8], G[:, cs], self.ident[0:8, 0:8])
            P.transpose(psA[:, 8:16], LB[:, cs], self.ident[0:8, 0:8])
            P.copy('dve', colA, psA[:, 0:16])
            gT, lbT = colA[:, 0:8], colA[:, 8:16]
            psB = self.bank()
            P.mm(psB[0:8, 0:128], gT, Ucum)
            psC = self.bank()
            P.mm(psC[:, 0:8], Ucum, gT)
            P.mm(psC[:, 8:16], Ublk, gT)
            P.mm(psC[:, 16:24], Ua, gT)
            P.mm(psC[:, 24:32], Ub, gT)
            P.act(GAMr, psB[0:8, 0:128], AF.Copy, scale=-1.0)
            P.copy('dve', ctmp[:, 0:8], lbT)
            P.tt('dve', ctmp[:, 8:16], psC[:, 0:8], lbT, ALU.add)
            gamc = self.Rview(WK + 9088, F32, [8])
            P.copy('dve', gamc, psC[:, 0:8])
            P.tt('dve', ctmp[:, 16:24], psC[:, 8:16], gamc, ALU.subtract)
            P.copy('dve', ctmp[:, 24:40], psC[:, 16:32])
            P.act(EXd, ctmp, AF.Exp)
            beta_c, cb_c, ct_c = EXd[:, 0:8], EXd[:, 8:16], EXd[:, 16:24]
            al_c = [EXd[:, 24:32], EXd[:, 32:40]]

            def bcol(col):
                return col.unsqueeze(2).to_broadcast([128, 8, 128])
            psk = self.bank(1).bitcast(BF16)
            for h in range(8):
                P.transpose(psk[:, h * 128:(h + 1) * 128], kT[:, h, cs], self.identb)
            pskv = psk.rearrange("p (h d) -> p h d", h=8)
            P.tt('dve', kb, pskv, bcol(cb_c), ALU.mult)
            P.tt('dve', ktd, pskv, bcol(ct_c), ALU.mult)
            psv_ = self.bank(1).bitcast(BF16)
            for h in range(8):
                P.transpose(psv_[:, h * 128:(h + 1) * 128], vT[:, h, cs], self.identb)
            P.tt('dve', bvd, psv_.rearrange("p (h d) -> p h d", h=8), bcol(beta_c), ALU.mult)
            return dict(al_c=al_c, kt=ktd, bv=bvd)

        phase = 99
        for dc in range(8):
            c0 = dc * 128
            cs = slice(c0, c0 + 128)
            if dc == 0:
                fr = front(0)
            al_c, kt, bv = fr['al_c'], fr['kt'], fr['bv']
            if phase < 3:
                continue
            BDg = self.Rview(W0 + 6144, F32, [8, 128], parts=8)
            BDe = self.Rview(W0 + 10240, F32, [8, 128], parts=8)
            P.tt('pool', BDg, Esel, GAMr.unsqueeze(1).to_broadcast([8, 8, 128]), ALU.mult)
            Eflat = Esel.rearrange("p h j -> p (h j)")
            BDgf = BDg.rearrange("p h j -> p (h j)")
            BDef = BDe.rearrange("p h j -> p (h j)")
            ones8 = self.onesf[0:8, :]
            psD = self.bank(2)
            for hf in range(2):
                fs = slice(hf * 512, (hf + 1) * 512)
                P.mm(psD[:, fs], ones8, BDgf[:, fs])
            psDv = psD.rearrange("p (h j) -> p h j", h=8)
            sub = 99
            if sub == 0:
                P.copy('dve', XA, psDv)
                continue
            pskk = self.bank(2)
            for h in range(8):
                P.mm(pskk[:, h * 128:(h + 1) * 128], kT[:, h, cs], kT[:, h, cs])
            psqk = self.bank(2)
            for h in range(8):
                P.mm(psqk[:, h * 128:(h + 1) * 128], qT[:, h, cs], kT[:, h, cs])
            mS = maskS.unsqueeze(1).to_broadcast([128, 8, 128])
            mI = maskI.unsqueeze(1).to_broadcast([128, 8, 128])
            egbc = tmpf
            P.act(egbc, psDv, AF.Exp, scale=-1.0)
            P.tt('dve', XA, psDv, mS, ALU.add)
            P.tt('dve', XA, XA, ctmp[:, 8:16].unsqueeze(2).to_broadcast([128, 8, 128]), ALU.add)
            P.act(XA, XA, AF.Exp)
            if sub == 1:
                continue
            A0 = Wv(26624, BF16, [8, 128])
            B0 = Wv(28672, BF16, [8, 128])
            P.stt('dve', A0, pskk.rearrange("p (h j) -> p h j", h=8), -1.0, XA, ALU.mult, ALU.mult)
            if sub == 2:
                continue
            P.tt('dve', XA, psDv, gamc.unsqueeze(2).to_broadcast([128, 8, 128]), ALU.add)
            if sub == 3:
                continue
            P.tt('dve', XA, XA, mI, ALU.add)
            if sub == 4:
                continue
            P.act(XA, XA, AF.Exp)
            if sub == 5:
                continue
            if sub == 6:
                P.copy('dve', XA, psqk.rearrange("p (h j) -> p h j", h=8))
                continue
            if sub == 7:
                P.copy('dve', qk, XA)
                continue
            if sub == 8:
                P.copy('act', qk, XA)
                continue
            if sub == 9:
                P.copy('dve', qk[:, :, 0:16], colA.unsqueeze(1).to_broadcast([128, 8, 16]))
                continue
            if sub == 10:
                P.copy('dve', kb, XA)
                continue
            P.tt('dve', qk, psqk.rearrange("p (h j) -> p h j", h=8), XA, ALU.mult)
            if phase < 4:
                continue
            pst = self.bank(1).bitcast(BF16)
            for h in range(8):
                P.transpose(pst[:, h * 128:(h + 1) * 128], A0[:, h, :], self.identb)
            pstv = pst.rearrange("p (h j) -> p h j", h=8)
            P.copy('act', B0, pstv)
            pst2 = self.bank(1).bitcast(BF16)
            for h in range(8):
                P.transpose(pst2[:, h * 128:(h + 1) * 128], qk[:, h, :], self.identb)
            P.copy('act', qkT, pst2.rearrange("p (h j) -> p h j", h=8))
            if phase < 5:
                continue
            idb = self.identb.unsqueeze(1).to_broadcast([128, 4, 128])
            st_ = []
            for hg in range(2):
                hs_ = slice(hg * 4, hg * 4 + 4)
                if hg == 0:
                    bufs = (Abuf, Bbuf, Tbuf)
                else:
                    bufs = ([self.Rview(W0 + 0, BF16, [4, 128]), self.Rview(W0 + 1024, BF16, [4, 128])],
                            [self.Rview(W0 + 2048, BF16, [4, 128]), self.Rview(W0 + 3072, BF16, [4, 128])],
                            [self.Rview(W0 + 4096, BF16, [4, 128]), self.Rview(W0 + 5120, BF16, [4, 128])])
                P.tt('dve', bufs[2][0], B0[:, hs_, :], idb, ALU.add)
                st_.append(dict(Ap=A0[:, hs_, :], Bp=B0[:, hs_, :], Tp=bufs[2][0], bufs=bufs, hs=hs_))
            for r in range(1, 7):
                for hg in range(2):
                    d_ = st_[hg]
                    Ap, Bp, Tp = d_['Ap'], d_['Bp'], d_['Tp']
                    An, Bn, Tn = d_['bufs'][0][r % 2], d_['bufs'][1][r % 2], d_['bufs'][2][r % 2]
                    if r <= 5:
                        pa = self.bank()
                        for h in range(4):
                            P.mm(pa[:, h * 128:(h + 1) * 128], Bp[:, h, :], Ap[:, h, :])
                    if r <= 4:
                        pb = self.bank()
                        for h in range(4):
                            P.mm(pb[:, h * 128:(h + 1) * 128], Ap[:, h, :], Bp[:, h, :])
                    if r >= 2:
                        pt = self.bank()
                        for h in range(4):
                            P.mm(pt[:, h * 128:(h + 1) * 128], Ap[:, h, :], Tp[:, h, :])
                    if r <= 5:
                        P.copy('act', An, pa.rearrange("p (h j) -> p h j", h=4))
                        d_['Ap'] = An
                    if r <= 4:
                        P.copy('act', Bn, pb.rearrange("p (h j) -> p h j", h=4))
                        d_['Bp'] = Bn
                    if r >= 2:
                        dst = Tn if r < 6 else TTf[:, d_['hs'], :]
                        P.tt('dve', dst, pt.rearrange("p (h j) -> p h j", h=4), Tp, ALU.add)
                        d_['Tp'] = dst
            if phase < 6:
                continue
            pswk = self.bank(2)
            for h in range(8):
                P.mm(pswk[:, h * 128:(h + 1) * 128], kb[:, h, :], TTf[:, h, :])
            P.act(nwkT, pswk.rearrange("p (h i) -> p h i", h=8), AF.Copy, scale=-1.0)
            P.tt('dve', qd, qT[:, :, cs], egbc, ALU.mult)
            if phase < 7:
                continue
            pso = self.psum[:, 6 * 512:8 * 512]
            for x in range(2):
                r_ = slice(64 * x, 64 * x + 64)
                psw = self.bank(2)
                for h in range(8):
                    P.mm(psw[r_, h * 128:(h + 1) * 128], TTf[r_, h, r_], bv[r_, h, :], start=True, stop=False)
                    P.mm(psw[r_, h * 128:(h + 1) * 128], nwkT[:, h, r_], Sb[:, h, :], start=False, stop=True)
                P.copy('act', w_sb[r_, :, :], psw[r_, :].rearrange("p (h d) -> p h d", h=8))
                for h in range(8):
                    oc = pso[:, h * 128 + 64 * x:h * 128 + 64 * x + 64]
                    P.mm(oc, Sb[:, h, :], qd[:, h, r_], start=True, stop=False)
                    P.mm(oc, w_sb[r_, h, :], qkT[r_, h, r_], start=False, stop=True)
                P.tt('dve', self.S, self.S, al_c[x].unsqueeze(2).to_broadcast([128, 8, 128]), ALU.mult)
                pss_ = self.bank(2)
                for h in range(8):
                    P.mm(pss_[:, h * 128:(h + 1) * 128], kt[r_, h, :], w_sb[r_, h, :])
                P.tt('dve', Sb, self.S, pss_.rearrange("p (h d) -> p h d", h=8), ALU.add)
                P.tt('dve', self.S, self.S, pss_.rearrange("p (h d) -> p h d", h=8), ALU.add)
                if x == 0 and dc + 1 < 8:
                    fr_next = front(dc + 1)
            finish(pso, c0, 128)
            if dc + 1 < 8:
                fr = fr_next
        if self.p == self.n_pass - 1:
            P.dma('sp', O['d2p'].rearrange("h k v -> k h v"), self.S)
        if self.p == 0 and dn_stop != 3:
            self.dn_samples(finish, Esel, G, LB, qT, kT, vT, Wv)
        self.bank_n = 8
        self.pbank = 0
        self.out_proj(w_out, 8, om)

    def dn_samples(self, finish, Esel, G, LB, qT, kT, vT, Wv):
        P, D, O = self.P, self.D, self.O
        sc = slice(1024, 1024 + NS)
        W0 = self.Woff[0] - self.Roff
        d16 = Wv(0, BF16, [NS, NS])
        kmask = Wv(512, BF16, [8, NS, NS])
        qmask = Wv(4608, BF16, [8, NS, NS])
        ab_tm = Wv(8704, F32, [16], parts=NS)
        v_tm = Wv(8768, F32, [8, 128], parts=NS)
        r_tm = Wv(12864, BF16, [8, 128], parts=NS)
        k_tm = Wv(14912, BF16, [8, 128], parts=NS)
        rmk = Wv(16960, BF16, [8, 128], parts=NS)
        t_tm = Wv(19008, F32, [8, 128], parts=NS)
        Sb0 = [Wv(23104, BF16, [8, 128]), Wv(25152, BF16, [8, 128])]
        Snb = Wv(27200, BF16, [8, 128])
        abc = Wv(29248, F32, [8, NS])
        S0 = [self.Rview(W0 + 0, F32, [8, 128]), self.Rview(W0 + 4096, F32, [8, 128])]
        Snew = [self.Rview(W0 + 8192, F32, [8, 128]), self.Rview(W0 + 12288, F32, [8, 128])]
        self.bank_n = 4
        self.pbank = 0
        pacc = self.psum[:, 4 * 512:6 * 512]
        pso = self.psum[:, 6 * 512:7 * 512]
        P.memset('pool', d16, 1.0)
        P.raw('pool', lambda e: e.affine_select(out=d16, in_=d16, pattern=[[1, NS], [-1, NS]],
                                               compare_op=ALU.is_equal, fill=0.0, base=0, channel_multiplier=0),
              [d16], [d16])
        d16b = d16.unsqueeze(1).to_broadcast([128, 8, NS, NS])
        P.tt('dve', kmask, kT[:, :, sc].unsqueeze(2).to_broadcast([128, 8, NS, NS]), d16b, ALU.mult)
        P.tt('dve', qmask, qT[:, :, sc].unsqueeze(2).to_broadcast([128, 8, NS, NS]), d16b, ALU.mult)
        ps = self.bank()
        P.transpose(ps[0:NS, 0:8], G[:, sc], self.ident[0:8, 0:8])
        P.transpose(ps[0:NS, 8:16], LB[:, sc], self.ident[0:8, 0:8])
        P.act(ab_tm, ps[0:NS, 0:16], AF.Exp)
        ps = self.bank()
        for h in range(8):
            P.mm(ps[:, h * NS:(h + 1) * NS], Esel[:, h, :], G[:, sc])
        P.act(abc, ps[:, 0:128].rearrange("p (h b) -> p h b", h=8), AF.Exp)
        pk = self.bank(1).bitcast(BF16)
        for h in range(8):
            P.transpose(pk[0:NS, h * 128:(h + 1) * 128], kT[:, h, sc], self.identb)
        P.copy('dve', k_tm, pk[0:NS, :].rearrange("p (h d) -> p h d", h=8))
        pv = self.bank(1).bitcast(BF16)
        for h in range(8):
            P.transpose(pv[0:NS, h * 128:(h + 1) * 128], vT[:, h, sc], self.identb)
        P.copy('dve', v_tm, pv[0:NS, :].rearrange("p (h d) -> p h d", h=8))
        paccv = pacc[0:NS, :].rearrange("p (h d) -> p h d", h=8)
        P.memset('dve', pacc[0:NS, :], 0.0)
        for b in range(NS):
            P.dma('sp', S0[b % 2], D['sd2'][b].rearrange("h k v -> k h v"))
            P.copy('act', Sb0[b % 2], S0[b % 2])
            for h in range(8):
                self.P.add('pe', (lambda o_, l_, r_: lambda e: e.matmul(o_, l_, r_, start=False, stop=False,
                                                                      skip_group_check=True))(
                    pacc[0:NS, h * 128:(h + 1) * 128], kmask[:, h, b, :], Sb0[b % 2][:, h, :]),
                    [kmask[:, h, b, :], Sb0[b % 2][:, h, :]], [pacc[0:NS, h * 128:(h + 1) * 128]])
        P.tt('dve', t_tm, paccv, ab_tm[:, 0:8].unsqueeze(2).to_broadcast([NS, 8, 128]), ALU.mult)
        P.tt('dve', t_tm, v_tm, t_tm, ALU.subtract)
        P.tt('dve', r_tm, t_tm, ab_tm[:, 8:16].unsqueeze(2).to_broadcast([NS, 8, 128]), ALU.mult)
        P.memset('dve', pacc[0:NS, :], 0.0)
        for b in range(NS):
            P.dma('sp', S0[b % 2], D['sd2'][b].rearrange("h k v -> k h v"))
            P.ts('dve', rmk, r_tm, self.ident[0:NS, b:b + 1], None, ALU.mult)
            pss_ = self.bank(2)
            for h in range(8):
                P.mm(pss_[:, h * 128:(h + 1) * 128], k_tm[:, h, :], rmk[:, h, :])
            Sn = Snew[b % 2]
            P.tt('dve', Sn, S0[b % 2], abc[:, :, b].unsqueeze(2).to_broadcast([128, 8, 128]), ALU.mult)
            P.tt('dve', Sn, Sn, pss_.rearrange("p (h d) -> p h d", h=8), ALU.add)
            P.dma('sp', O['d2s'][b].rearrange("h k v -> k h v"), Sn)
            P.copy('act', Snb, Sn)
            for h in range(8):
                self.P.add('pe', (lambda o_, l_, r_: lambda e: e.matmul(o_, l_, r_, start=False, stop=False,
                                                                      skip_group_check=True))(
                    pacc[0:NS, h * 128:(h + 1) * 128], qmask[:, h, b, :], Snb[:, h, :]),
                    [qmask[:, h, b, :], Snb[:, h, :]], [pacc[0:NS, h * 128:(h + 1) * 128]])
        P.copy('dve', t_tm, paccv)
        for h in range(8):
            P.transpose(pso[:, h * NS:(h + 1) * NS], t_tm[:, h, :], self.ident[0:NS, 0:NS])
        finish(pso[:, 0:8 * NS], 1024, NS)

    def final_out(self, p):
        P, O = self.P, self.O
        ss = self.Rview(16384, F32, [2])
        junk = self.Rview(20480, F32, [1024])
        self.fgain_bc = self.Rview(24576, F32, [1024])
        P.dma('sp', self.fgain_bc, self.D['final_norm'].rearrange("(o n) -> o n", o=1).to_broadcast([128, 1024]))
        for tc in range(8 + (1 if p == 0 else 0)):
            if tc < 8:
                c0, n = tc * 128, 128
                dst = O['yp'][p * PASS + tc * 128: p * PASS + (tc + 1) * 128, :]
            else:
                c0, n = 1024, NS
                dst = O['ys']
            ps = self.bank(2)
            for k in range(8):
                P.transpose(ps[0:n, k * 128:(k + 1) * 128], self.xT[:, k, c0:c0 + n], self.ident)
            P.act(junk[0:n, :], ps[0:n, :], AF.Square, accum_out=ss[0:n, 0:1])
            P.act(ss[0:n, 1:2], ss[0:n, 0:1], AF.Sqrt, scale=1.0 / 1024, bias=self.epsc[0:n, 0:1])
            P.recip(ss[0:n, 1:2], ss[0:n, 1:2])
            yo = self.Rview((tc % 2) * 4096, F32, [1024])
            P.stt('dve', yo[0:n, :], ps[0:n, :], ss[0:n, 1:2], self.fgain_bc[0:n, :], ALU.mult, ALU.mult)
            P.dma('sp', dst, yo[0:n, :])


_CACHE = {}


def _get_nc(n_layers=4, n_pass=2):
    key = (n_layers, n_pass)
    if key not in _CACHE:
        _CACHE[key] = Builder(n_layers, n_pass).build()
    return _CACHE[key]


def make_in_maps(inp):
    f = lambda a: np.ascontiguousarray(np.asarray(a, dtype=np.float32))
    shared = {}
    shared['l0_norm'] = f(inp['l0_norm'])
    shared['l1_norm'] = f(inp['l1_norm'])
    shared['l2_norm'] = f(inp['l2_norm'])
    shared['l3_norm'] = f(inp['l3_norm'])
    shared['final_norm'] = f(inp['final_norm'])
    shared['l0_w_in'] = f(inp['l0_pool_w_in'])
    shared['l0_w_grp'] = f(inp['l0_pool_w_grp'])
    shared['l0_scale'] = f(inp['l0_pool_scale'])
    shared['l0_w_out'] = f(inp['l0_pool_w_out'])
    shared['l3_w_in'] = f(inp['l3_pool_w_in'])
    shared['l3_w_grp'] = f(inp['l3_pool_w_grp'])
    shared['l3_scale'] = f(inp['l3_pool_scale'])
    shared['l3_w_out'] = f(inp['l3_pool_w_out'])
    shared['l1_w_in'] = f(inp['l1_sgu_w_in'])
    shared['l1_ln_g'] = f(inp['l1_sgu_ln_g'])
    shared['l1_ln_b'] = f(inp['l1_sgu_ln_b'])
    shared['l1_w_s'] = f(inp['l1_sgu_w_s'])
    shared['l1_b_s'] = f(inp['l1_sgu_b_s'])
    shared['l1_w_out'] = f(inp['l1_sgu_w_out'])
    shared['l2_w_in'] = f(inp['l2_dn_w_in'])
    shared['l2_conv_w'] = f(inp['l2_dn_conv_w'])
    shared['l2_a_log'] = f(inp['l2_dn_a_log'])
    shared['l2_dt_bias'] = f(inp['l2_dn_dt_bias'])
    shared['l2_o_gain'] = f(inp['l2_dn_o_gain'])
    shared['l2_w_out'] = f(inp['l2_dn_w_out'])
    maps = []
    for c in range(8):
        m = dict(shared)
        m['xp'] = f(inp['x_prompt'][c])
        m['xs'] = f(inp['x_sample'][c * NS:(c + 1) * NS, 0, :])
        m['sp0'] = f(inp['state_pool_l0'][c * NS:(c + 1) * NS])
        m['sc2'] = f(inp['state_conv_l2'][c * NS:(c + 1) * NS])
        m['sd2'] = f(inp['state_delta_l2'][c * NS:(c + 1) * NS])
        m['sp3'] = f(inp['state_pool_l3'][c * NS:(c + 1) * NS])
        maps.append(m)
    return maps


def gather(res):
    r = res.results
    cat = lambda k: np.concatenate([np.asarray(r[c][k]) for c in range(8)], axis=0)
    stk = lambda k: np.stack([np.asarray(r[c][k]) for c in range(8)], axis=0)
    return (
        stk('yp').astype(np.float32),
        cat('ys').reshape(128, 1, 1024).astype(np.float32),
        stk('p0p').astype(np.float32),
        cat('p0s').astype(np.float32),
        cat('v1s').reshape(128, 1, 2048).astype(np.float32),
        stk('c2p').astype(np.float32),
        cat('c2s').astype(np.float32),
        stk('d2p').astype(np.float32),
        cat('d2s').astype(np.float32),
        stk('p3p').astype(np.float32),
        cat('p3s').astype(np.float32),
    )


def kernel(**inputs):
    nc = _get_nc()
    res = run_bass_kernel_spmd(nc, make_in_maps(inputs), core_ids=list(range(8)))
    return gather(res)
```
